# Optimizing a Trainium2 kernel written in Bass

```python
import math
import jax
import jax.numpy as jnp
from jax import lax
import numpy as np

D_MODEL = 1024
BATCH = 16
SEQ = 2048
DEPTH = 1

PLE_DIM = 256
MIX_WIDTH = D_MODEL
ATTN_WIDTH = MIX_WIDTH // 2
ATTN_HEAD_DIM = 64
ATTN_HEADS = ATTN_WIDTH // ATTN_HEAD_DIM
ATTN_PATTERNS = ((128, 1), (512, 4), (2048, 16))
ROPE_THETA = 500000.0
ROT_DIM = ATTN_HEAD_DIM // 4
MLSTM_WIDTH = MIX_WIDTH - ATTN_WIDTH
MLSTM_HEADS = 4
MLSTM_HEAD_DIM = MLSTM_WIDTH // MLSTM_HEADS
MLSTM_CHUNK = 64
MLSTM_CONV = 3
FFN_DIM = ((8 * D_MODEL // 3) + 255) // 256 * 256
FFN_CONV = 3
EPS = 1e-6
IN_WIDTHS = (ATTN_WIDTH,) * 3 + (MLSTM_WIDTH,) * 3 + (MLSTM_HEADS,) * 4
IN_DIM = sum(IN_WIDTHS)
IN_SPLITS = tuple(int(s) for s in np.cumsum(IN_WIDTHS)[:-1])

kernel_name = "hybrid_dilated_attn_mlstm_convffn_encoder"


def rmsnorm(x, g):
    xf = x.astype(jnp.float32)
    y = xf * lax.rsqrt(jnp.mean(jnp.square(xf), axis=-1, keepdims=True) + EPS)
    return (y * g.astype(jnp.float32)).astype(x.dtype)


def dwconv_centred(x, w, b):
    K = w.shape[0]
    S = x.shape[1]
    pad = K // 2
    xp = jnp.pad(x, ((0, 0), (pad, pad), (0, 0)))
    out = xp[:, 0:S, :] * w[0]
    for j in range(1, K):
        out = out + xp[:, j:j + S, :] * w[j]
    return out + b


def partial_rotary(t, positions):
    half = ROT_DIM // 2
    inv = jnp.power(ROPE_THETA, -jnp.arange(0, ROT_DIM, 2, dtype=jnp.float32) / ROT_DIM)
    ang = positions.astype(jnp.float32)[..., None] * inv
    cos = jnp.cos(ang)[:, :, None, :]
    sin = jnp.sin(ang)[:, :, None, :]
    t1 = t[..., :half]
    t2 = t[..., half:ROT_DIM]
    return jnp.concatenate([t1 * cos - t2 * sin, t2 * cos + t1 * sin, t[..., ROT_DIM:]], axis=-1)


def banded_attention(q, k, v, radius):
    B, H, R, M, hd = q.shape
    blk = radius
    nb = -(-M // blk)
    Mp = nb * blk
    qb = jnp.pad(q, ((0, 0),) * 3 + ((0, Mp - M), (0, 0))).reshape(B, H, R, nb, blk, hd)

    def windows(t):
        tp = jnp.pad(t, ((0, 0),) * 3 + ((blk, Mp - M + blk), (0, 0))).reshape(B, H, R, nb + 2, blk, hd)
        return jnp.concatenate([tp[:, :, :, j:j + nb] for j in range(3)], axis=-2)

    kw = windows(k)
    vw = windows(v)
    qpos = jnp.arange(Mp).reshape(nb, blk)
    kpos = (jnp.arange(nb)[:, None] - 1) * blk + jnp.arange(3 * blk)[None, :]
    valid = ((jnp.abs(qpos[:, :, None] - kpos[:, None, :]) <= radius)
             & (kpos[:, None, :] >= 0) & (kpos[:, None, :] < M))
    s = jnp.einsum('bhrnqd,bhrnkd->bhrnqk', qb, kw)
    s = jnp.where(valid, s, -jnp.inf)
    lse = jax.nn.logsumexp(s, axis=-1)
    o = jnp.einsum('bhrnqk,bhrnkd->bhrnqd', jnp.exp(s - lse[..., None]), vw)
    return o.reshape(B, H, R, Mp, hd)[:, :, :, :M], lse.reshape(B, H, R, Mp)[:, :, :, :M]


def dilated_branch(q, k, v, dil, n_side):
    B, H, S, hd = q.shape
    M = S // dil

    def to_res(t):
        return t.reshape(B, H, M, dil, hd).transpose(0, 1, 3, 2, 4)

    o, lse = banded_attention(to_res(q), to_res(k), to_res(v), n_side)
    return (o.transpose(0, 1, 3, 2, 4).reshape(B, H, S, hd),
            lse.transpose(0, 1, 3, 2).reshape(B, H, S))


def dilated_mixture_attention(q, k, v, positions):
    B, S, H, hd = q.shape
    q = partial_rotary(q, positions) * (hd ** -0.5)
    k = partial_rotary(k, positions)
    q, k, v = (t.transpose(0, 2, 1, 3) for t in (q, k, v))
    outs, lses = [], []
    for window, dil in ATTN_PATTERNS:
        o, l = dilated_branch(q, k, v, dil, (window // 2) // dil)
        outs.append(o)
        lses.append(l)
    wts = jax.nn.softmax(jnp.stack(lses, axis=0), axis=0)
    out = jnp.sum(wts[..., None] * jnp.stack(outs, axis=0), axis=0)
    return out.transpose(0, 2, 1, 3).reshape(B, S, H * hd)


def mlstm_scan(q, k, v, i_pre, f_pre):
    B, H, S, d = q.shape
    L = MLSTM_CHUNK
    nc = S // L

    def chunks(t):
        return jnp.moveaxis(t.reshape((B, H, nc, L) + t.shape[3:]), 2, 0)

    logf = jax.nn.log_sigmoid(f_pre)
    tril = jnp.tril(jnp.ones((L, L), dtype=bool))

    def step(carry, inp):
        C, n, m = carry
        qc, kc, vc, ic, lf = inp
        b = jnp.cumsum(lf, axis=-1)
        dmat = jnp.where(tril, b[..., :, None] - b[..., None, :] + ic[..., None, :], -jnp.inf)
        m_inter = b + m[..., None]
        m_t = jnp.maximum(m_inter, jnp.max(dmat, axis=-1))
        pmat = jnp.einsum('bhtd,bhsd->bhts', qc, kc) * jnp.exp(dmat - m_t[..., None])
        a = jnp.exp(m_inter - m_t)
        num = a[..., None] * jnp.einsum('bhtd,bhde->bhte', qc, C) + jnp.einsum('bhts,bhse->bhte', pmat, vc)
        den = a * jnp.einsum('bhtd,bhd->bht', qc, n) + jnp.sum(pmat, axis=-1)
        h = num / jnp.maximum(jnp.abs(den), jnp.exp(-m_t))[..., None]
        g = b[..., -1]
        logw = g[..., None] - b + ic
        m_new = jnp.maximum(g + m, jnp.max(logw, axis=-1))
        decay = jnp.exp(g + m - m_new)
        w = jnp.exp(logw - m_new[..., None])
        C = decay[..., None, None] * C + jnp.einsum('bhs,bhsd,bhse->bhde', w, kc, vc)
        n = decay[..., None] * n + jnp.einsum('bhs,bhsd->bhd', w, kc)
        return (C, n, m_new), h

    init = (jnp.zeros((B, H, d, d), jnp.float32), jnp.zeros((B, H, d), jnp.float32),
            jnp.zeros((B, H), jnp.float32))
    _, h = lax.scan(step, init, (chunks(q), chunks(k), chunks(v), chunks(i_pre), chunks(logf)))
    return jnp.moveaxis(h, 0, 2).reshape(B, H, S, d)


def mlstm_mixer(u, v, o, i_f, i_b, f_f, f_b, conv_w, conv_b, w_q, w_k, b_ig, b_fg, gn_g):
    B, S, _ = u.shape
    H, d = MLSTM_HEADS, MLSTM_HEAD_DIM
    uc = jax.nn.silu(dwconv_centred(u, conv_w, conv_b)).reshape(B, S, H, d)
    q = jnp.einsum('bshd,hde->bhse', uc, w_q)
    k = jnp.einsum('bshd,hde->bhse', uc, w_k) * (d ** -0.5)
    vh = v.reshape(B, S, H, d).transpose(0, 2, 1, 3)

    def gate(g, b):
        return (g + b).transpose(0, 2, 1)

    def flip(t):
        return jnp.flip(t, axis=2)

    h_fwd = mlstm_scan(q, k, vh, gate(i_f, b_ig[0]), gate(f_f, b_fg[0]))
    h_bwd = flip(mlstm_scan(flip(q), flip(k), flip(vh), flip(gate(i_b, b_ig[1])), flip(gate(f_b, b_fg[1]))))
    hs = h_fwd + h_bwd
    mu = jnp.mean(hs, axis=-1, keepdims=True)
    var = jnp.mean(jnp.square(hs - mu), axis=-1, keepdims=True)
    hn = ((hs - mu) * lax.rsqrt(var + EPS)).transpose(0, 2, 1, 3).reshape(B, S, H * d)
    return hn * gn_g * jax.nn.sigmoid(o)


def conv_glu_ffn(h, w_gate, conv_w, conv_b, w_up, w_down):
    a = dwconv_centred(h @ w_gate, conv_w, conv_b)
    return (jax.nn.gelu(a, approximate=False) * (h @ w_up)) @ w_down


def setup_inputs(seed: int = 0) -> dict:
    key = jax.random.key(seed)
    ks = jax.random.split(key, 32)

    def nrm(k, shape, scale):
        return jax.random.normal(k, shape, jnp.float32) * scale

    def gain(k, shape):
        return 1.0 + nrm(k, shape, 0.02)

    H = MLSTM_HEADS
    fg_bias = jnp.broadcast_to(jnp.linspace(3.0, 6.0, H, dtype=jnp.float32), (DEPTH, 2, H))
    return {
        'x': nrm(ks[0], (BATCH, SEQ, D_MODEL), 1.0),
        'p': nrm(ks[1], (DEPTH, BATCH, SEQ, PLE_DIM), 1.0),
        'positions': jnp.broadcast_to(jnp.arange(SEQ, dtype=jnp.int32), (BATCH, SEQ)),
        'ln_mix_g': gain(ks[2], (DEPTH, D_MODEL)),
        'w_in': nrm(ks[3], (DEPTH, D_MODEL, IN_DIM), D_MODEL ** -0.5),
        'mlstm_conv_w': nrm(ks[4], (DEPTH, MLSTM_CONV, MLSTM_WIDTH), MLSTM_CONV ** -0.5),
        'mlstm_conv_b': nrm(ks[5], (DEPTH, MLSTM_WIDTH), 0.02),
        'w_mq': nrm(ks[6], (DEPTH, H, MLSTM_HEAD_DIM, MLSTM_HEAD_DIM), MLSTM_HEAD_DIM ** -0.5),
        'w_mk': nrm(ks[7], (DEPTH, H, MLSTM_HEAD_DIM, MLSTM_HEAD_DIM), MLSTM_HEAD_DIM ** -0.5),
        'b_igate': nrm(ks[8], (DEPTH, 2, H), 0.1),
        'b_fgate': fg_bias + nrm(ks[9], (DEPTH, 2, H), 0.1),
        'mlstm_gn_g': gain(ks[10], (DEPTH, MLSTM_WIDTH)),
        'w_out': nrm(ks[11], (DEPTH, MIX_WIDTH, D_MODEL), MIX_WIDTH ** -0.5),
        'ln_ffn_g': gain(ks[12], (DEPTH, D_MODEL)),
        'w_ffn_gate': nrm(ks[13], (DEPTH, D_MODEL, FFN_DIM), D_MODEL ** -0.5),
        'ffn_conv_w': nrm(ks[14], (DEPTH, FFN_CONV, FFN_DIM), FFN_CONV ** -0.5),
        'ffn_conv_b': nrm(ks[15], (DEPTH, FFN_DIM), 0.02),
        'w_ffn_up': nrm(ks[16], (DEPTH, D_MODEL, FFN_DIM), D_MODEL ** -0.5),
        'w_ffn_down': nrm(ks[17], (DEPTH, FFN_DIM, D_MODEL), FFN_DIM ** -0.5),
        'ln_ple_g': gain(ks[18], (DEPTH, D_MODEL)),
        'w_ple_gate': nrm(ks[19], (DEPTH, D_MODEL, D_MODEL), D_MODEL ** -0.5),
        'b_ple_gate': nrm(ks[20], (DEPTH, D_MODEL), 0.02),
        'w_ple_proj': nrm(ks[21], (DEPTH, PLE_DIM, D_MODEL), PLE_DIM ** -0.5),
        'ln_final_g': gain(ks[22], (D_MODEL,)),
    }


def reference(x, p, positions, ln_mix_g, w_in, mlstm_conv_w, mlstm_conv_b, w_mq, w_mk,
              b_igate, b_fgate, mlstm_gn_g, w_out, ln_ffn_g, w_ffn_gate, ffn_conv_w,
              ffn_conv_b, w_ffn_up, w_ffn_down, ln_ple_g, w_ple_gate, b_ple_gate,
              w_ple_proj, ln_final_g):
    B, S, _ = x.shape
    f32 = jnp.float32
    for i in range(DEPTH):
        h = rmsnorm(x, ln_mix_g[i])
        z = (h @ w_in[i]).astype(f32)
        q_a, k_a, v_a, u_m, v_m, o_m, i_f, i_b, f_f, f_b = jnp.split(z, IN_SPLITS, axis=-1)
        attn = dilated_mixture_attention(
            q_a.reshape(B, S, ATTN_HEADS, ATTN_HEAD_DIM),
            k_a.reshape(B, S, ATTN_HEADS, ATTN_HEAD_DIM),
            v_a.reshape(B, S, ATTN_HEADS, ATTN_HEAD_DIM), positions)
        mem = mlstm_mixer(u_m, v_m, o_m, i_f, i_b, f_f, f_b,
                          mlstm_conv_w[i].astype(f32), mlstm_conv_b[i].astype(f32),
                          w_mq[i].astype(f32), w_mk[i].astype(f32),
                          b_igate[i].astype(f32), b_fgate[i].astype(f32),
                          mlstm_gn_g[i].astype(f32))
        mixed = jnp.concatenate([attn, mem], axis=-1).astype(x.dtype)
        x = x + mixed @ w_out[i]
        h = rmsnorm(x, ln_ffn_g[i])
        x = x + conv_glu_ffn(h, w_ffn_gate[i], ffn_conv_w[i], ffn_conv_b[i], w_ffn_up[i], w_ffn_down[i])
        h = rmsnorm(x, ln_ple_g[i])
        x = x + (p[i] @ w_ple_proj[i]) * jax.nn.sigmoid(h @ w_ple_gate[i] + b_ple_gate[i])
    return rmsnorm(x, ln_final_g)
```

```python
import math
from contextlib import ExitStack
import numpy as np
import ml_dtypes
import concourse.bass as bass
import concourse.mybir as mybir
from concourse.bass_utils import run_bass_kernel_spmd

F32 = mybir.dt.float32
BF16 = mybir.dt.bfloat16
I32 = mybir.dt.int32
U8 = mybir.dt.uint8
AF = mybir.ActivationFunctionType
ALU = mybir.AluOpType

ENG_NAMES = ['sync', 'scalar', 'vector', 'gpsimd', 'tensor']
S_LEN = 2048
NSEQ = 2
EPS = 1e-6


class Sched:
    def __init__(self, nc):
        self.nc = nc
        self.ops = {e: [] for e in ENG_NAMES}
        self.nops = {e: 0 for e in ENG_NAMES}
        self.last_w = {}
        self.readers = {}
        self.waited = {e: {} for e in ENG_NAMES}
        self.dma_cnt = {}
        self.signal = {e: set() for e in ENG_NAMES}

    def _need(self, eng, tok, raw=False):
        if tok is None:
            return
        kind, src, idx = tok
        if kind == 'eng' and src == eng:
            if not raw or eng == 'tensor' or eng == 'sync':
                return
        key = (kind, src)
        if self.waited[eng].get(key, 0) >= idx:
            return
        self.waited[eng][key] = idx
        self.ops[eng].append(('wait', tok))
        if kind == 'eng':
            self.signal[src].add(idx)

    def _deps(self, eng, reads, writes):
        for k in reads:
            self._need(eng, self.last_w.get(k), raw=True)
        for k in writes:
            self._need(eng, self.last_w.get(k))
            for t in self.readers.get(k, ()):
                self._need(eng, t)

    def _commit(self, tok, reads, writes):
        for k in reads:
            self.readers.setdefault(k, []).append(tok)
        for k in writes:
            self.last_w[k] = tok
            self.readers[k] = []

    def op(self, eng, fn, reads=(), writes=()):
        self._deps(eng, reads, writes)
        self.nops[eng] += 1
        idx = self.nops[eng]
        self.ops[eng].append(('op', fn, idx))
        tok = ('eng', eng, idx)
        self._commit(tok, reads, writes)
        return tok

    def dma(self, eng, fn, stream, reads=(), writes=()):
        self._deps(eng, reads, writes)
        cnt = self.dma_cnt.get(stream, 0) + 1
        self.dma_cnt[stream] = cnt
        self.ops[eng].append(('dma', fn, stream))
        tok = ('dma', stream, cnt)
        self._commit(tok, reads, writes)
        return tok

    def wait_all(self, eng, keys):
        for k in keys:
            self._need(eng, self.last_w.get(k))
            for t in self.readers.get(k, ()):
                self._need(eng, t)

    def barrier(self):
        toks = [('eng', e, self.nops[e]) for e in ENG_NAMES if self.nops[e] > 0]
        toks += [('dma', s, c) for s, c in self.dma_cnt.items()]
        for e in ENG_NAMES:
            for t in toks:
                self._need(e, t)

    def emit(self):
        nc = self.nc
        with ExitStack() as st:
            sems = {}
            for e in ENG_NAMES:
                sems[('eng', e)] = st.enter_context(nc.semaphore('s_' + e))
            for s in self.dma_cnt:
                sems[('dma', s)] = st.enter_context(nc.semaphore('d_' + s))
            rank = {}
            for e in ENG_NAMES:
                for r, idx in enumerate(sorted(self.signal[e])):
                    rank[(e, idx)] = r + 1
            block = st.enter_context(nc.Block())

            def make(e):
                def body(engh):
                    for item in self.ops[e]:
                        if item[0] == 'wait':
                            kind, src, idx = item[1]
                            if kind == 'eng':
                                engh.wait_ge(sems[('eng', src)], rank[(src, idx)])
                            else:
                                engh.wait_ge(sems[('dma', src)], 16 * idx)
                        elif item[0] == 'op':
                            ins = item[1](engh)
                            if (e, item[2]) in rank:
                                ins.then_inc(sems[('eng', e)], 1)
                        else:
                            ins = item[1](engh)
                            ins.then_inc(sems[('dma', item[2])], 16)
                return body
            for e in ENG_NAMES:
                getattr(block, e)(make(e))


def build_program(dbg=None, nseq=NSEQ, stop_after=None):
    nc = bass.Bass("TRN2", target_bir_lowering=False)
    S = Sched(nc)

    def din(name, shape, dt=F32):
        return nc.dram_tensor(name, list(shape), dt, kind="ExternalInput").ap()

    x_d = din("x", [NSEQ, S_LEN, 1024])
    p_d = din("p", [NSEQ, S_LEN, 256])
    pos_d = din("pos", [NSEQ, 128, 16], I32)
    wina_d = din("win_a", [4, 128, 8, 384])
    winm_d = din("win_m", [4, 128, 8, 384])
    wing_d = din("win_g", [128, 8, 16])
    wmq_d = din("wmq", [128, 4, 128])
    wmk_d = din("wmk", [128, 4, 128])
    wout_d = din("wout", [2, 128, 4, 1024])
    wgu_d = din("wgu", [22, 128, 8, 256])
    wfd_d = din("wfd", [8, 128, 22, 128])
    wpg_d = din("wpg", [8, 128, 8, 128])
    wpp_d = din("wpp", [128, 2, 1024])
    gains_d = din("gains", [128, 4, 8])
    bpg_d = din("bpg", [128, 8])
    fcw_d = din("fcw", [128, 22, 3])
    fcb_d = din("fcb", [128, 22])
    mcw_d = din("mcw", [128, 4, 3])
    mcb_d = din("mcb", [128, 4])
    gng_d = din("gng", [128, 512])
    gngT_d = din("gngT", [128, 4])
    gbias_d = din("gbias", [128, 16])
    invf_d = din("invf", [128, 8])
    identb_d = din("identb", [128, 128], BF16)
    identf_d = din("identf", [128, 128])
    maskc_d = din("maskc", [128, 23 * 128], BF16)
    masku_d = din("masku", [128, 128], BF16)
    maskl_d = din("maskl", [128, 128], BF16)
    triu_d = din("triu", [128, 128])
    tril_d = din("tril", [128, 128])
    y_d = nc.dram_tensor("y", [NSEQ, S_LEN, 1024], F32, kind="ExternalOutput").ap()
    scr_d = nc.dram_tensor("rden_scr", [8, 512], F32).ap()
    scr2_d = nc.dram_tensor("rden_scr2", [8, 512], F32).ap()
    dbg_d = {}
    if dbg:
        for name, shape in dbg.items():
            dbg_d[name] = nc.dram_tensor("dbg_" + name, list(shape), F32, kind="ExternalOutput").ap()

    ARENA = 207 * 1024
    arena = nc.alloc_sbuf_tensor("arena", [128, ARENA], U8)
    cur = [0]
    marks = {}

    def alloc(shape, dt, name=None):
        esz = {F32: 4, BF16: 2, I32: 4}[dt]
        n = int(np.prod(shape[1:]))
        nb = n * esz
        off = (cur[0] + 63) // 64 * 64
        assert off + nb <= ARENA, (name, off, nb, ARENA)
        cur[0] = off + nb
        v = arena[:, off:off + nb].bitcast(dt)
        if len(shape) == 3:
            v = v.rearrange('p (a b) -> p a b', a=shape[1])
        elif len(shape) == 4:
            v = v.rearrange('p (a b c) -> p a b c', a=shape[1], b=shape[2])
        return v

    psb = [nc.alloc_psum_tensor("ps%d" % i, [128, 512], F32) for i in range(8)]
    rot = {'A': [0, 1, 2, 3], 'B': [4, 5, 6, 7], 'S': [0, 1, 2, 3, 6, 7]}
    rotc = {'A': 0, 'B': 0, 'S': 0}

    def bank(group='A'):
        i = rot[group][rotc[group] % len(rot[group])]
        rotc[group] += 1
        return i

    def PS(i):
        return psb[i][:]

    def PSB(i):
        return psb[i][:].bitcast(BF16)

    def pk(i):
        return 'ps%d' % i

    def act(out, in_, func, reads, writes, **kw):
        return S.op('scalar', lambda e: e.activation(out=out, in_=in_, func=func, **kw), reads, writes)

    def mm(out, lhsT, rhs, start, stop, reads, writes):
        return S.op('tensor', lambda e: e.matmul(out, lhsT=lhsT, rhs=rhs, start=start, stop=stop), reads, writes)

    def tr(out, in_, ident, reads, writes):
        return S.op('tensor', lambda e: e.transpose(out=out, in_=in_, identity=ident), reads, writes)

    def tt(eng, out, in0, in1, op, reads, writes):
        return S.op(eng, lambda e: e.tensor_tensor(out=out, in0=in0, in1=in1, op=op), reads, writes)

    def ts(eng, out, in0, s1, s2, op0, op1, reads, writes):
        if op1 is None:
            return S.op(eng, lambda e: e.tensor_scalar(out=out, in0=in0, scalar1=s1, scalar2=None, op0=op0), reads, writes)
        return S.op(eng, lambda e: e.tensor_scalar(out=out, in0=in0, scalar1=s1, scalar2=s2, op0=op0, op1=op1), reads, writes)

    def stt(out, in0, scalar, in1, op0, op1, reads, writes):
        return S.op('vector', lambda e: e.scalar_tensor_tensor(out=out, in0=in0, scalar=scalar, in1=in1, op0=op0, op1=op1), reads, writes)

    def cp(eng, out, in_, reads, writes):
        if eng == 'scalar':
            return S.op(eng, lambda e: e.copy(out=out, in_=in_), reads, writes)
        return S.op(eng, lambda e: e.tensor_copy(out=out, in_=in_), reads, writes)

    def recip(out, in_, reads, writes):
        return S.op('vector', lambda e: e.reciprocal(out=out, in_=in_), reads, writes)

    def bnstats(out, in_, reads, writes):
        return S.op('vector', lambda e: e.bn_stats(out=out, in_=in_), reads, writes)

    def bnaggr(out, in_, reads, writes):
        return S.op('vector', lambda e: e.bn_aggr(out=out, in_=in_), reads, writes)

    def memset(eng, ap, val, writes):
        return S.op(eng, lambda e: e.memset(ap, val), (), writes)

    def ld(out, in_, stream, writes, reads=()):
        return S.dma('sync', lambda e: e.dma_start(out=out, in_=in_), stream, reads, writes)

    def ldc(out, in_, stream, writes, reads=()):
        return S.dma('gpsimd', lambda e: e.dma_start(out=out, in_=in_), stream, reads, writes)

    def st(out, in_, stream, reads):
        return S.dma('sync', lambda e: e.dma_start(out=out, in_=in_), stream, reads, ())

    def dump(name, ap, key):
        if name in dbg_d:
            S.dma('gpsimd', lambda e: e.dma_start(out=dbg_d[name], in_=ap), 'dbg_' + name, [key], ())

    xT = alloc([128, 8, S_LEN], F32)
    hT = alloc([128, 8, S_LEN], BF16)
    identb = alloc([128, 128], BF16)
    identf = alloc([128, 128], F32)
    maskc = alloc([128, 23 * 128], BF16)
    masku = alloc([128, 128], BF16)
    maskl = alloc([128, 128], BF16)
    triu = alloc([128, 128], F32)
    tril = alloc([128, 128], F32)
    onesf = alloc([128, 128], F32)
    onesb = alloc([128, 128], BF16)
    gains = alloc([128, 4, 8], F32)
    bpg = alloc([128, 8], F32)
    fcw = alloc([128, 22, 3], F32)
    fcb = alloc([128, 22], F32)
    mcw = alloc([128, 4, 3], F32)
    mcb = alloc([128, 4], F32)
    gngT = alloc([128, 4], F32)
    gbias = alloc([128, 16], F32)
    invf = alloc([128, 8], F32)
    epsc = alloc([128, 1], F32)
    onec = alloc([128, 1], F32)
    sq = [alloc([128, 512], BF16) for _ in range(2)]
    rs = [alloc([128, 512], F32) for _ in range(2)]
    wmq = alloc([128, 4, 128], BF16)
    wmk = alloc([128, 4, 128], BF16)
    phase_base = cur[0]

    consts = [(identb, identb_d, 'identb'), (identf, identf_d, 'identf'), (maskc, maskc_d, 'maskc'),
              (masku, masku_d, 'masku'), (maskl, maskl_d, 'maskl'), (triu, triu_d, 'triu'), (tril, tril_d, 'tril'),
              (gains, gains_d, 'gains'), (bpg, bpg_d, 'bpg'), (fcw, fcw_d, 'fcw'), (fcb, fcb_d, 'fcb'),
              (mcw, mcw_d, 'mcw'), (mcb, mcb_d, 'mcb'), (gngT, gngT_d, 'gngT'), (gbias, gbias_d, 'gbias'),
              (invf, invf_d, 'invf')]
    for sb, d, nm in consts:
        ld(sb, d, 'c_' + nm, [nm])
    ldc(wmq, wmq_d, 'c_wmq', ['wmq'])
    ldc(wmk, wmk_d, 'c_wmk', ['wmk'])
    memset('vector', onesf, 1.0, ['onesf'])
    memset('vector', onesb, 1.0, ['onesb'])
    memset('vector', epsc, EPS, ['epsc'])
    memset('vector', onec, 1.0, ['onec'])

    evac_rr = [0]

    def evac_eng():
        evac_rr[0] += 1
        return 'scalar' if evac_rr[0] % 2 else 'vector'

    def norm_to_hT(gi):
        for blk in range(4):
            tsl = slice(512 * blk, 512 * blk + 512)
            b = bank('A')
            for c in range(8):
                sqb = sq[c % 2]
                act(sqb, xT[:, c, tsl], AF.Square, ['xT'], ['sq%d' % (c % 2)])
                mm(PS(b), onesb, sqb, c == 0, c == 7, ['onesb', 'sq%d' % (c % 2)], [pk(b)])
            r = rs[blk % 2]
            rk = 'rs%d' % (blk % 2)
            act(r, PS(b), AF.Ln, [pk(b), 'epsc'], [rk], scale=1.0 / 1024.0, bias=epsc)
            act(r, r, AF.Exp, [rk], [rk], scale=-0.5)
            for c in range(8):
                if gi is None:
                    continue
                stt(hT[:, c, tsl], xT[:, c, tsl], gains[:, gi, c:c + 1], r, ALU.mult, ALU.mult,
                    ['xT', 'gains', rk], ['hT'])
            yield blk, r, rk

    for b_ in range(nseq):
        if stop_after == 'consts':
            dump('xT', onesf[:, 0:8].unsqueeze(1).broadcast_to([128, 8, 8]) if False else xT[:, :, 0:256], 'onesf')
            break
        cur[0] = phase_base
        if b_ > 0:
            cur[0] = phase_base + 2 * 4096
        NXS = 6
        xs = [alloc([128, 1024], F32) for _ in range(NXS)]
        for i in range(16):
            xsb = xs[i % NXS]
            k = 'xs%d' % (i % NXS)
            if b_ > 0:
                S.dma('gpsimd', lambda e, o_=xsb, i_=x_d[b_, 128 * i:128 * i + 128, :]: e.dma_start(out=o_, in_=i_), k, (), [k])
            else:
                ld(xsb, x_d[b_, 128 * i:128 * i + 128, :], k, [k])
            for g in range(2):
                bk = bank('A')
                for c4 in range(4):
                    c = 4 * g + c4
                    tr(PS(bk)[:, 128 * c4:128 * c4 + 128], xsb[:, 128 * c:128 * c + 128], identf, [k, 'identf'], [pk(bk)])
                cp(evac_eng(), xT[:, 4 * g:4 * g + 4, 128 * i:128 * i + 128],
                   PS(bk).rearrange('p (c t) -> p c t', c=4), [pk(bk)], ['xT', 'xTf%d' % (i // 4)])
        S.barrier()
        cur[0] = phase_base
        if b_ == 0:
            dump('xT', xT[:, :, 0:256], 'xT')

        if stop_after == 'load':
            break
        for _ in norm_to_hT(0):
            pass
        if stop_after == 'norm':
            break
        mixT = alloc([128, 4, S_LEN], BF16)
        wu = [alloc([128, 8, 384], BF16) for _ in range(2)]
        unit_base = cur[0]

        def load_wu(u):
            slot = u % 2
            src = wina_d[u] if u < 4 else winm_d[u - 4]
            for hh in range(2):
                ldc(wu[slot][:, 4 * hh:4 * hh + 4, :].rearrange('p a b -> p (a b)'),
                    src[:, 4 * hh:4 * hh + 4, :].rearrange('p a b -> p (a b)'),
                    'wu%d_%d' % (slot, hh), ['wu%d_%d' % (slot, hh)])

        def load_wo(half):
            for hh in range(2):
                ldc(wo[:, 2 * hh:2 * hh + 2, :].rearrange('p a b -> p (a b)'),
                    wout_d[half, :, 2 * hh:2 * hh + 2, :].rearrange('p a b -> p (a b)'),
                    'wo_%d' % hh, ['wo_%d' % hh])

        def apply_wout():
            for n in range(8):
                for blk in range(4):
                    tsl = slice(512 * blk, 512 * blk + 512)
                    bk = bank('B')
                    for u in range(4):
                        mm(PS(bk), wo[:, u, 128 * n:128 * n + 128], mixT[:, u, tsl], u == 0, u == 3,
                           ['wo_0', 'wo_1', 'mixT'], [pk(bk)])
                    tt('vector', xT[:, n, tsl], PS(bk), xT[:, n, tsl], ALU.add, [pk(bk), 'xT'], ['xT'])

        load_wu(0)

        posi = alloc([128, 16], I32)
        posf = alloc([128, 16], F32)
        ang = alloc([128, 16, 8], F32)
        rt1 = alloc([128, 16, 8], F32)
        rt2 = alloc([128, 16, 8], F32)
        cosT = alloc([128, 16, 8], F32)
        sinT = alloc([128, 16, 8], F32)
        ld(posi, pos_d[b_], 'posi', ['posi'])
        cp('vector', posf, posi, ['posi'], ['posf'])
        for j in range(8):
            ts('vector', ang[:, :, j], posf, invf[:, j:j + 1], None, ALU.mult, None, ['posf', 'invf'], ['ang'])
        TWO_PI = 2.0 * math.pi
        MAGIC = 12582912.0
        for which, shift, dst, dk in ((0, 0.0, sinT, 'sinT'), (1, math.pi / 2, cosT, 'cosT')):
            ts('vector', rt1, ang, 1.0 / TWO_PI, shift / TWO_PI, ALU.mult, ALU.add, ['ang'], ['rt1'])
            ts('vector', rt1, rt1, MAGIC, None, ALU.add, None, ['rt1'], ['rt1'])
            ts('vector', rt1, rt1, MAGIC, None, ALU.subtract, None, ['rt1'], ['rt1'])
            ts('vector', rt2, ang, shift, None, ALU.add, None, ['ang'], ['rt2'])
            stt(rt2, rt1, -TWO_PI, rt2, ALU.mult, ALU.add, ['rt1', 'rt2'], ['rt2'])
            ts('vector', rt2, rt2, 3.14159, -3.14159, ALU.min, ALU.max, ['rt2'], ['rt2'])
            act(dst, rt2, AF.Sin, ['rt2'], [dk])
        att_base = cur[0]
        if b_ == 0:
            dump('cosT', cosT, 'cosT')
            dump('sinT', sinT, 'sinT')
            dump('ang', ang, 'ang')
        if stop_after == 'rot':
            break

        att_end = [0]
        deferred = []
        DEFER1 = 18
        DEFER2 = 18

        def tick_deferred(flush=False):
            for item in list(deferred):
                item[0] -= 1
                if item[0] <= 0 or flush:
                    item[2](*item[1])
                    deferred.remove(item)

        for hp in range(4):
            if hp == 1:
                cur[0] = att_end[0]
                wo = alloc([128, 4, 1024], BF16)
                load_wo(0)
            cur[0] = att_base
            slot = hp % 2
            wuk = ['wu%d_0' % slot, 'wu%d_1' % slot]
            load_wu(hp + 1)
            QTz = [alloc([128, S_LEN], BF16) for _ in range(2)]
            KT = alloc([128, S_LEN], BF16)
            V1 = alloc([128, 16, 2, 128], BF16)
            rdf = alloc([128, 512], F32)
            Rs = [alloc([128, 512], F32) for _ in range(4)]
            dcol = [alloc([128, 4], F32) for _ in range(2)]
            qk = [alloc([128, 4, 64], F32) for _ in range(2)]
            ra = alloc([128, 4, 8], F32)
            rb = alloc([128, 4, 8], F32)
            ra2 = alloc([128, 4, 8], F32)
            rb2 = alloc([128, 4, 8], F32)
            Eb = [alloc([128, 512], BF16) for _ in range(6)]
            Pb = [alloc([128, 512], BF16) for _ in range(6)]
            att_end[0] = max(att_end[0], cur[0])
            memset('gpsimd', V1, 0.0, ['V1'])
            memset('gpsimd', V1[:, :, 0, 64:65], 1.0, ['V1'])
            memset('gpsimd', V1[:, :, 1, 0:1], 1.0, ['V1'])
            memset('gpsimd', QTz[0][64:128, :], 0.0, ['QT'])
            memset('gpsimd', QTz[1][0:64, :], 0.0, ['QT'])

            def proj_front(i):
                tsl = slice(128 * i, 128 * i + 128)
                zb = bank('A')
                for kc in range(8):
                    mm(PS(zb)[:, 0:384], hT[:, kc, tsl], wu[slot][:, kc, :], kc == 0, kc == 7, ['hT'] + wuk, [pk(zb)])
                z4 = PS(zb)[:, 0:256].rearrange('p (a b) -> p a b', a=4)
                qkb = qk[i % 2]
                qkk = 'qk%d' % (i % 2)
                cb = cosT[:, i:i + 1, :].broadcast_to([128, 4, 8])
                sb_ = sinT[:, i:i + 1, :].broadcast_to([128, 4, 8])
                t1 = z4[:, :, 0:8]
                t2 = z4[:, :, 8:16]
                tt('vector', ra, t1, cb, ALU.mult, [pk(zb), 'cosT'], ['ra'])
                tt('vector', rb, t2, sb_, ALU.mult, [pk(zb), 'sinT'], ['rb'])
                tt('vector', ra2, t2, cb, ALU.mult, [pk(zb), 'cosT'], ['ra2'])
                tt('vector', rb2, t1, sb_, ALU.mult, [pk(zb), 'sinT'], ['rb2'])
                tt('vector', qkb[:, :, 0:8], ra, rb, ALU.subtract, ['ra', 'rb'], [qkk])
                tt('vector', qkb[:, :, 8:16], ra2, rb2, ALU.add, ['ra2', 'rb2'], [qkk])
                cp('scalar', qkb[:, :, 16:64], z4[:, :, 16:64], [pk(zb)], [qkk])
                cp('scalar', V1[:, i, 0, 0:64], PS(zb)[:, 256:320], [pk(zb)], ['V1'])
                cp('scalar', V1[:, i, 1, 64:128], PS(zb)[:, 320:384], [pk(zb)], ['V1'])

            def proj_back(i):
                tsl = slice(128 * i, 128 * i + 128)
                qkb = qk[i % 2]
                qkk = 'qk%d' % (i % 2)
                qk2 = qkb.rearrange('p a b -> p (a b)')
                tb = bank('A')
                tb2 = bank('A')
                tr(PS(tb)[:, 0:128], qk2[:, 0:128], identf, [qkk, 'identf'], [pk(tb)])
                tr(PS(tb2)[:, 0:128], qk2[:, 128:256], identf, [qkk, 'identf'], [pk(tb2)])
                cp('vector', QTz[0][0:64, tsl], PS(tb)[0:64, 0:128], [pk(tb)], ['QT'])
                cp('vector', QTz[1][64:128, tsl], PS(tb)[64:128, 0:128], [pk(tb)], ['QT'])
                cp('scalar', KT[:, tsl], PS(tb2)[:, 0:128], [pk(tb2)], ['KT'])

            for i in range(17):
                if i < 16:
                    proj_front(i)
                if i >= 1:
                    proj_back(i - 1)
                if i >= 3:
                    tick_deferred()
                    tick_deferred()
            if stop_after == 'proj':
                break
            if b_ == 0 and hp == 0:
                dump('KT', KT[:, 0:512], 'KT')
            accb = [4, 5, 6, 7]
            units = []
            gi = 0
            for QC in range(4):
                for hh in range(2):
                    kbs = [kb for kb in range(16) if any(abs(4 * QC + j - kb) <= 8 for j in range(4))]
                    for kb in kbs:
                        units.append((QC, hh, kb, kb == kbs[0], kb == kbs[-1], gi))
                    gi += 1
            NBUF = len(Eb)
            LOOK = 4

            def vcols(QC, kb):
                js = [j for j in range(4) if abs(4 * QC + j - kb) <= 8]
                return 128 * js[0], 128 * (js[-1] + 1)

            def front(n):
                QC, hh, kb, first, last, g = units[n]
                c0, c1 = vcols(QC, kb)
                sb2 = bank('A')
                mm(PS(sb2)[:, c0:c1], KT[:, 128 * kb:128 * kb + 128], QTz[hh][:, 512 * QC + c0:512 * QC + c1], True, True,
                   ['KT', 'QT'], [pk(sb2)])
                e_ = Eb[n % NBUF]
                ek = 'E%d' % (n % NBUF)
                p_ = Pb[n % NBUF]
                pk_ = 'P%d' % (n % NBUF)
                act(e_[:, c0:c1], PS(sb2)[:, c0:c1], AF.Exp, [pk(sb2)], [ek], scale=0.125)
                d0 = 4 * QC - kb + 11
                tt('vector', p_[:, c0:c1], e_[:, c0:c1], maskc[:, 128 * d0 + c0:128 * d0 + c1], ALU.mult, [ek, 'maskc'], [pk_])

            def back(n):
                QC, hh, kb, first, last, g = units[n]
                qsl = slice(512 * QC, 512 * QC + 512)
                p_ = Pb[n % NBUF]
                pk_ = 'P%d' % (n % NBUF)
                ab_ = accb[g % 4]
                c0, c1 = vcols(QC, kb)
                if hh == 0:
                    mm(PS(ab_)[0:65, c0:c1], V1[:, kb, 0, 0:65], p_[:, c0:c1], first, last, [pk_, 'V1'], [pk(ab_)])
                else:
                    mm(PS(ab_)[:, c0:c1], V1[:, kb, 1, :], p_[:, c0:c1], first, last, [pk_, 'V1'], [pk(ab_)])
                if last:
                    pr = 64 if hh == 0 else 0
                    rows = slice(0, 64) if hh == 0 else slice(64, 128)
                    prs = slice(pr, pr + 1)
                    cp('vector', rdf[prs, :], PS(ab_)[prs, :], [pk(ab_)], ['rdf'])
                    g8 = g % 8
                    dc_ = dcol[g % 2]
                    dck = 'dcol%d' % (g % 2)
                    rs_ = Rs[g % 4]
                    rsk = 'Rs%d' % (g % 4)
                    S.dma('sync', lambda e, o_=scr_d[g8:g8 + 1, :], i_=rdf[prs, :]: e.dma_start(out=o_, in_=i_),
                          'scrw%d' % g8, ['rdf'], ['scrA%d' % g8])
                    S.dma('sync', lambda e, o_=dc_, i_=scr_d[g8, :].rearrange('(p f) -> p f', f=4): e.dma_start(out=o_, in_=i_),
                          'dcr%d' % (g % 2), ['scrA%d' % g8], [dck])
                    deferred.append([DEFER1, (hp, hh, g, qsl, ab_, prs, rows), finalize_s1])

            def finalize_s1(hp_, hh, g, qsl, ab_, prs, rows, Rs=Rs, dcol=dcol):
                g8 = g % 8
                dc_ = dcol[g % 2]
                dck = 'dcol%d' % (g % 2)
                rs_ = Rs[g % 4]
                rsk = 'Rs%d' % (g % 4)
                recip(dc_, dc_, [dck], [dck])
                S.dma('sync', lambda e, o_=scr2_d[g8, :].rearrange('(p f) -> p f', f=4), i_=dc_: e.dma_start(out=o_, in_=i_),
                      'dcw%d' % g8, [dck], ['scrB%d' % g8])
                S.dma('sync', lambda e, o_=rs_[rows, :], i_=scr2_d[g8:g8 + 1, :].broadcast_to([64, 512]): e.dma_start(out=o_, in_=i_),
                      'scrr%d' % (g % 4), ['scrB%d' % g8], [rsk])
                deferred.append([DEFER2, (hp_, hh, g, qsl, ab_, prs, rows), finalize_pe])

            def finalize_pe(hp_, hh, g, qsl, ab_, prs, rows, Rs=Rs):
                rs_ = Rs[g % 4]
                rsk = 'Rs%d' % (g % 4)
                tt('vector', mixT[rows, hp_, qsl], PS(ab_)[rows, :], rs_[rows, :], ALU.mult, [pk(ab_), rsk], ['mixT'])

            for n in range(len(units) + LOOK):
                if n < len(units):
                    front(n)
                if n >= LOOK:
                    back(n - LOOK)
                tick_deferred()
        if stop_after == 'proj':
            break
        while deferred:
            tick_deferred(flush=True)
        if b_ == 0:
            dump('mixA', mixT[:, :, 0:256], 'mixT')
        apply_wout()
        S.barrier()
        if stop_after == 'attn':
            break

        cur[0] = unit_base
        wg = alloc([128, 8, 16], BF16)
        gates = alloc([128, 16, 16], F32)
        e1 = alloc([128, 16, 8], F32)
        lf = [alloc([128, 64], F32) for _ in range(2)]
        igs = [alloc([128, 64], F32) for _ in range(2)]
        ea = [alloc([128, 64], F32) for _ in range(2)]
        eb = [alloc([128, 64], F32) for _ in range(2)]
        eg = [alloc([128, 64], F32) for _ in range(2)]
        ldc(wg.rearrange('p a b -> p (a b)'), wing_d.rearrange('p a b -> p (a b)'), 'wg', ['wg'])
        gb = bank('A')
        for i in range(16):
            for kc in range(8):
                mm(PS(gb)[:, 16 * i:16 * i + 16], hT[:, kc, 128 * i:128 * i + 128], wg[:, kc, :], kc == 0, kc == 7,
                   ['hT', 'wg'], [pk(gb)])
        tt('vector', gates, PS(gb)[:, 0:256].rearrange('p (a b) -> p a b', a=16),
           gbias.unsqueeze(1).broadcast_to([128, 16, 16]), ALU.add, [pk(gb), 'gbias'], ['gates'])
        act(e1, gates[:, :, 8:16], AF.Exp, ['gates'], ['e1'], scale=-1.0)
        for d in range(2):
            act(lf[d].rearrange('p (a b) -> p a b', a=16), e1[:, :, 4 * d:4 * d + 4], AF.Ln, ['e1', 'onec'], ['lf%d' % d],
                scale=1.0, bias=onec)
        cb_ = bank('A')
        tri = [triu, tril]
        for d in range(2):
            mm(PS(cb_)[:, 64 * d:64 * d + 64], tri[d], lf[d], True, True, ['triu', 'tril', 'lf%d' % d], [pk(cb_)])
        for d in range(2):
            mm(PS(cb_)[:, 128 + 64 * d:128 + 64 * d + 64], onesf, lf[d], True, True, ['onesf', 'lf%d' % d], [pk(cb_)])
        for d in range(2):
            tt('vector', igs[d].rearrange('p (a b) -> p a b', a=16), gates[:, :, 4 * d:4 * d + 4],
               PS(cb_)[:, 64 * d:64 * d + 64].rearrange('p (a b) -> p a b', a=16), ALU.add, ['gates', pk(cb_)], ['igs%d' % d])
            act(ea[d], igs[d], AF.Exp, ['igs%d' % d], ['ea%d' % d])
            act(eb[d], PS(cb_)[:, 64 * d:64 * d + 64], AF.Exp, [pk(cb_)], ['eb%d' % d], scale=-1.0)
            act(eg[d], PS(cb_)[:, 128 + 64 * d:128 + 64 * d + 64], AF.Exp, [pk(cb_)], ['eg%d' % d], scale=-1.0)
        munit_base = cur[0]
        if b_ == 0:
            dump('gates', gates, 'gates')
            for d_ in range(2):
                dump('lf%d' % d_, lf[d_], 'lf%d' % d_)
                dump('ea%d' % d_, ea[d_], 'ea%d' % d_)
                dump('eb%d' % d_, eb[d_], 'eb%d' % d_)
                dump('eg%d' % d_, eg[d_], 'eg%d' % d_)

        for h in range(4):
            cur[0] = munit_base
            u = 4 + h
            slot = u % 2
            wuk = ['wu%d_0' % slot, 'wu%d_1' % slot]
            if h < 3:
                load_wu(u + 1)
            u_sb = alloc([128, S_LEN + 2], F32)
            ctmp = [alloc([128, 512], F32) for _ in range(2)]
            sgt = [alloc([128, 512], BF16) for _ in range(2)]
            ucT = alloc([128, S_LEN], BF16)
            qT = alloc([128, S_LEN], BF16)
            kT = alloc([128, S_LEN], BF16)
            ktok = alloc([128, 16, 128], BF16)
            Vp = [alloc([128, 16, 129], BF16) for _ in range(2)]
            sigo = alloc([128, 16, 128], BF16)
            Cbf = [alloc([128, 16, 129], BF16) for _ in range(2)]
            Y = [alloc([128, 129], F32) for _ in range(2)]
            PTm = [alloc([128, 128], BF16) for _ in range(4)]
            dn = alloc([128, 32], F32)
            dneg = alloc([128, 32], F32)
            rr = alloc([128, 32], F32)
            s1 = alloc([128, 16], F32)
            s2 = alloc([128, 16], F32)
            rstd = alloc([128, 16], F32)
            hn = [alloc([128, 128], F32) for _ in range(2)]
            mem = [alloc([128, 128], F32) for _ in range(2)]
            memset('vector', u_sb[:, 0:1], 0.0, ['u_sb'])
            memset('vector', u_sb[:, S_LEN + 1:S_LEN + 2], 0.0, ['u_sb'])
            for blk in range(4):
                tsl = slice(512 * blk, 512 * blk + 512)
                bk = bank('A')
                for kc in range(8):
                    mm(PS(bk), wu[slot][:, kc, 0:128], hT[:, kc, tsl], kc == 0, kc == 7, ['hT'] + wuk, [pk(bk)])
                cp('scalar', u_sb[:, 1 + 512 * blk:1 + 512 * blk + 512], PS(bk), [pk(bk)], ['u_sb'])
            def conv_op(k):
                blk, step = divmod(k, 3)
                o = 512 * blk
                ct = ctmp[blk % 2]
                ck = 'ct%d' % (blk % 2)
                if step == 0:
                    ts('vector', ct, u_sb[:, 1 + o:1 + o + 512], mcw[:, h, 1:2], mcb[:, h:h + 1], ALU.mult, ALU.add,
                       ['u_sb', 'mcw', 'mcb'], [ck])
                elif step == 1:
                    stt(ct, u_sb[:, o:o + 512], mcw[:, h, 0:1], ct, ALU.mult, ALU.add, ['u_sb', 'mcw', ck], [ck])
                else:
                    stt(ct, u_sb[:, 2 + o:2 + o + 512], mcw[:, h, 2:3], ct, ALU.mult, ALU.add, ['u_sb', 'mcw', ck], [ck])
                    sg_ = sgt[blk % 2]
                    sgk_ = 'sgt%d' % (blk % 2)
                    act(sg_, ct, AF.Sigmoid, [ck], [sgk_])
                    tt('vector', ucT[:, o:o + 512], ct, sg_, ALU.mult, [ck, sgk_], ['ucT'])

            def qk_block(blk):
                tsl = slice(512 * blk, 512 * blk + 512)
                bk = bank('A')
                mm(PS(bk), wmq[:, h, :], ucT[:, tsl], True, True, ['wmq', 'ucT'], [pk(bk)])
                cp('vector', qT[:, tsl], PS(bk), [pk(bk)], ['qT'])
                bk = bank('A')
                mm(PS(bk), wmk[:, h, :], ucT[:, tsl], True, True, ['wmk', 'ucT'], [pk(bk)])
                act(kT[:, tsl], PS(bk), AF.Copy, [pk(bk)], ['kT'], scale=128.0 ** -0.5)
                bk = bank('A')
                for i4 in range(4):
                    i = 4 * blk + i4
                    mm(PS(bk)[:, 128 * i4:128 * i4 + 128], ucT[:, 128 * i:128 * i + 128], wmk[:, h, :], True, True,
                       ['wmk', 'ucT'], [pk(bk)])
                act(ktok[:, 4 * blk:4 * blk + 4, :], PS(bk).rearrange('p (a b) -> p a b', a=4), AF.Copy, [pk(bk)], ['ktok'],
                    scale=128.0 ** -0.5)

            qk_at = {6: 0, 9: 1, 12: 2}
            for i in range(16):
                tsl = slice(128 * i, 128 * i + 128)
                bk = bank('A')
                for kc in range(8):
                    mm(PS(bk)[:, 0:256], hT[:, kc, tsl], wu[slot][:, kc, 128:384], kc == 0, kc == 7, ['hT'] + wuk, [pk(bk)])
                act(Vp[0][:, i, 0:128], PS(bk)[:, 0:128], AF.Copy, [pk(bk), 'ea0'], ['Vp0'], scale=ea[0][:, 4 * i + h:4 * i + h + 1])
                act(sigo[:, i, :], PS(bk)[:, 128:256], AF.Sigmoid, [pk(bk)], ['sigo'])
                ts('vector', Vp[1][:, i, 0:128], PS(bk)[:, 0:128], ea[1][:, 4 * i + h:4 * i + h + 1], None, ALU.mult, None,
                   [pk(bk), 'ea1', 'sigo', 'Vp0'], ['Vp1'])
                if i < 12:
                    conv_op(i)
                if i in qk_at:
                    qk_block(qk_at[i])
            qk_block(3)
            for d in range(2):
                cp('vector', Vp[d][:, :, 128:129], ea[d].rearrange('p (a b) -> p a b', a=16)[:, :, h:h + 1], ['ea%d' % d], ['Vp%d' % d])
            if b_ == 0 and h == 0:
                dump('ucT', ucT[:, 0:512], 'ucT')
                dump('qT', qT[:, 0:512], 'qT')
                dump('kT', kT[:, 0:512], 'kT')
                dump('ktok', ktok[:, 0:4, :], 'ktok')
                dump('Vp0', Vp[0][:, 0:4, :], 'Vp0')
                dump('sigo', sigo[:, 0:4, :], 'sigo')
            orders = [list(range(0, 15)), list(range(15, 0, -1))]
            prevs = [None, None]
            for k in range(15):
                for d in range(2):
                    c = orders[d][k]
                    prev = prevs[d]
                    bk = bank('B')
                    mm(PS(bk)[:, 0:129], ktok[:, c, :], Vp[d][:, c, :], True, True, ['ktok', 'Vp%d' % d], [pk(bk)])
                    if prev is None:
                        cp('vector', Y[d], PS(bk)[:, 0:129], [pk(bk)], ['Y%d' % d])
                    else:
                        stt(Y[d], Y[d], eg[d][:, 4 * prev + h:4 * prev + h + 1], PS(bk)[:, 0:129], ALU.mult, ALU.add,
                            ['Y%d' % d, 'eg%d' % d, pk(bk)], ['Y%d' % d])
                    act(Cbf[d][:, c, :], Y[d], AF.Copy, ['Y%d' % d, 'eg%d' % d], ['Cbf%d' % d], scale=eg[d][:, 4 * c + h:4 * c + h + 1])
                    prevs[d] = c
            ebh = [eb[d].rearrange('p (c g) -> p c g', g=4)[:, :, h] for d in range(2)]
            hsS = u_sb[:, 0:2048].rearrange('p (c e) -> p c e', c=16)

            def st_and_masks(c):
                csl = slice(128 * c, 128 * c + 128)
                sbk = bank('A')
                mm(PS(sbk)[:, 0:128], kT[:, csl], qT[:, csl], True, True, ['kT', 'qT'], [pk(sbk)])
                tt('vector', PTm[2 * (c % 2)], PS(sbk)[:, 0:128], masku, ALU.mult, [pk(sbk), 'masku'], ['PTm%d' % (2 * (c % 2))])
                tt('vector', PTm[2 * (c % 2) + 1], PS(sbk)[:, 0:128], maskl, ALU.mult, [pk(sbk), 'maskl'], ['PTm%d' % (2 * (c % 2) + 1)])

            dbk = bank('B')
            st_and_masks(0)
            for c in range(16):
                csl = slice(128 * c, 128 * c + 128)
                if c + 1 < 16:
                    st_and_masks(c + 1)
                for d in range(2):
                    nb = c - 1 if d == 0 else c + 1
                    has = 0 <= nb <= 15
                    pt = PTm[2 * (c % 2) + d]
                    ptk = 'PTm%d' % (2 * (c % 2) + d)
                    col = 16 * d + c
                    mm(PS(dbk)[:, col:col + 1], pt, Vp[d][:, c, 128:129], True, not has, [ptk, 'Vp%d' % d], [pk(dbk)])
                    if has:
                        mm(PS(dbk)[:, col:col + 1], qT[:, csl], Cbf[d][:, nb, 128:129], False, True, ['qT', 'Cbf%d' % d], [pk(dbk)])
            for d in range(2):
                tt('vector', dn[:, 16 * d:16 * d + 16], PS(dbk)[:, 16 * d:16 * d + 16], ebh[d], ALU.mult, [pk(dbk), 'eb%d' % d], ['dn'])
            ts('vector', dneg, dn, -1.0, None, ALU.mult, None, ['dn'], ['dneg'])
            tt('vector', dn, dn, onesf[:, 0:32], ALU.max, ['dn', 'onesf'], ['dn'])
            tt('vector', dn, dn, dneg, ALU.max, ['dn', 'dneg'], ['dn'])
            recip(dn, dn, ['dn'], ['dn'])
            for d in range(2):
                tt('vector', rr[:, 16 * d:16 * d + 16], dn[:, 16 * d:16 * d + 16], ebh[d], ALU.mult, ['dn', 'eb%d' % d], ['rr'])
            st_and_masks(0)
            for c in range(16):
                csl = slice(128 * c, 128 * c + 128)
                if c + 1 < 16:
                    st_and_masks(c + 1)
                ab = []
                for d in range(2):
                    bk = bank('B')
                    ab.append(bk)
                    nb = c - 1 if d == 0 else c + 1
                    has = 0 <= nb <= 15
                    pt = PTm[2 * (c % 2) + d]
                    ptk = 'PTm%d' % (2 * (c % 2) + d)
                    mm(PS(bk)[:, 0:128], pt, Vp[d][:, c, 0:128], True, not has, [ptk, 'Vp%d' % d], [pk(bk)])
                    if has:
                        mm(PS(bk)[:, 0:128], qT[:, csl], Cbf[d][:, nb, 0:128], False, True, ['qT', 'Cbf%d' % d], [pk(bk)])
                act(hsS[:, c, :], PS(ab[0])[:, 0:128], AF.Copy, [pk(ab[0]), 'rr'], ['u_sb'], scale=rr[:, c:c + 1])
                stt(hsS[:, c, :], PS(ab[1])[:, 0:128], rr[:, 16 + c:17 + c], hsS[:, c, :], ALU.mult, ALU.add,
                    [pk(ab[1]), 'rr', 'u_sb'], ['u_sb'])
            sqv = ucT.rearrange('p (c e) -> p c e', c=16)
            act(sqv, hsS, AF.Square, ['u_sb'], ['ucT'])
            S.op('vector', lambda e, hsS=hsS, s1=s1: e.tensor_reduce(out=s1, in_=hsS, axis=mybir.AxisListType.X, op=ALU.add), ['u_sb'], ['s1'])
            ts('vector', s1, s1, 1.0 / 128.0, None, ALU.mult, None, ['s1'], ['s1'])
            tt('vector', dneg[:, 0:16], s1, s1, ALU.mult, ['s1'], ['dneg'])
            S.op('vector', lambda e, sqv=sqv, s2=s2: e.tensor_reduce(out=s2, in_=sqv, axis=mybir.AxisListType.X, op=ALU.add), ['ucT'], ['s2'])
            stt(s2, s2, 1.0 / 128.0, dneg[:, 0:16], ALU.mult, ALU.subtract, ['s2', 'dneg'], ['s2'])
            act(rstd, s2, AF.Sqrt, ['s2', 'epsc'], ['rstd'], scale=1.0, bias=epsc)
            recip(rstd, rstd, ['rstd'], ['rstd'])
            stt(dneg[:, 16:32], s1, -1.0, rstd, ALU.mult, ALU.mult, ['s1', 'rstd'], ['dneg'])
            if b_ == 0 and h == 0:
                dump('hs', hsS[:, 1, :], 'u_sb')
                dump('rr0', rr[:, 1:2], 'rr')
                dump('rr1', rr[:, 17:18], 'rr')
            tbs = {}

            def ln_a(c):
                act(hn[c % 2], hsS[:, c, :], AF.Identity, ['u_sb', 'rstd', 'dneg'], ['hn%d' % (c % 2)],
                    scale=rstd[:, c:c + 1], bias=dneg[:, 16 + c:17 + c])
                tt('vector', mem[c % 2], hn[c % 2], sigo[:, c, :], ALU.mult, ['hn%d' % (c % 2), 'sigo'], ['mem%d' % (c % 2)])
                tb = bank('A')
                tbs[c] = tb
                tr(PS(tb)[:, 0:128], mem[c % 2], identf, ['mem%d' % (c % 2), 'identf'], [pk(tb)])

            def ln_b(c):
                csl = slice(128 * c, 128 * c + 128)
                tb = tbs[c]
                act(mixT[:, h, csl], PS(tb)[:, 0:128], AF.Copy, [pk(tb), 'gngT'], ['mixT'], scale=gngT[:, h:h + 1])

            for c in range(18):
                if c < 16:
                    ln_a(c)
                if c >= 2:
                    ln_b(c - 2)
        if b_ == 0:
            dump('mixM', mixT[:, :, 0:256], 'mixT')
        S.barrier()
        cur[0] = unit_base
        wo = alloc([128, 4, 1024], BF16)
        load_wo(1)
        apply_wout()
        S.barrier()
        if b_ == 0:
            dump('x1', xT[:, :, 0:256], 'xT')
        if stop_after == 'mixer':
            break

        cur[0] = phase_base
        for _ in norm_to_hT(1):
            pass
        actT = alloc([128, 22, 1024], BF16)
        wgu = [alloc([128, 8, 256], BF16) for _ in range(2)]
        wd = [alloc([128, 22, 128], BF16) for _ in range(2)]
        g_sb = [alloc([128, 1026], F32) for _ in range(2)]
        ftmp = alloc([128, 1024], F32)
        ge = [alloc([128, 1024], BF16) for _ in range(2)]

        def load_wgu(f):
            ldc(wgu[f % 2].rearrange('p a b -> p (a b)'), wgu_d[f].rearrange('p a b -> p (a b)'), 'wgu%d' % (f % 2), ['wgu%d' % (f % 2)])

        def load_wd(n):
            for hh in range(2):
                ldc(wd[n % 2][:, 11 * hh:11 * hh + 11, :].rearrange('p a b -> p (a b)'),
                    wfd_d[n, :, 11 * hh:11 * hh + 11, :].rearrange('p a b -> p (a b)'),
                    'wd%d_%d' % (n % 2, hh), ['wd%d_%d' % (n % 2, hh)])

        for half in range(2):
            t0 = 1024 * half
            load_wgu(0)
            for f in range(22):
                if f < 21:
                    load_wgu(f + 1)
                elif True:
                    load_wd(0)
                w = wgu[f % 2]
                wk = 'wgu%d' % (f % 2)
                gs = g_sb[f % 2]
                gk = 'g_sb%d' % (f % 2)
                for b2 in range(2):
                    tsl = slice(t0 + 512 * b2, t0 + 512 * b2 + 512)
                    bk = bank('A')
                    for kc in range(8):
                        mm(PS(bk), w[:, kc, 0:128], hT[:, kc, tsl], kc == 0, kc == 7, ['hT', wk], [pk(bk)])
                    cp('scalar', gs[:, 1 + 512 * b2:1 + 512 * b2 + 512], PS(bk), [pk(bk)], [gk])
                bk = bank('A')
                for kc in range(8):
                    mm(PS(bk)[:, 0:2], w[:, kc, 0:128], hT[:, kc, 1023:1025], kc == 0, kc == 7, ['hT', wk], [pk(bk)])
                if half == 0:
                    memset('vector', gs[:, 0:1], 0.0, [gk])
                    cp('vector', gs[:, 1025:1026], PS(bk)[:, 1:2], [pk(bk)], [gk])
                else:
                    cp('vector', gs[:, 0:1], PS(bk)[:, 0:1], [pk(bk)], [gk])
                    memset('vector', gs[:, 1025:1026], 0.0, [gk])
                ts('vector', ftmp, gs[:, 1:1025], fcw[:, f, 1:2], fcb[:, f:f + 1], ALU.mult, ALU.add, [gk, 'fcw', 'fcb'], ['ftmp'])
                stt(ftmp, gs[:, 0:1024], fcw[:, f, 0:1], ftmp, ALU.mult, ALU.add, [gk, 'fcw', 'ftmp'], ['ftmp'])
                stt(ftmp, gs[:, 2:1026], fcw[:, f, 2:3], ftmp, ALU.mult, ALU.add, [gk, 'fcw', 'ftmp'], ['ftmp'])
                gb_ = ge[f % 2]
                gek = 'ge%d' % (f % 2)
                act(gb_, ftmp, AF.Gelu, ['ftmp'], [gek])
                for b2 in range(2):
                    tsl = slice(t0 + 512 * b2, t0 + 512 * b2 + 512)
                    bk = bank('B')
                    for kc in range(8):
                        mm(PS(bk), w[:, kc, 128:256], hT[:, kc, tsl], kc == 0, kc == 7, ['hT', wk], [pk(bk)])
                    tt('vector', actT[:, f, 512 * b2:512 * b2 + 512], PS(bk), gb_[:, 512 * b2:512 * b2 + 512], ALU.mult,
                       [pk(bk), gek], ['actT'])
            for n in range(8):
                if n < 7:
                    load_wd(n + 1)
                wdk = ['wd%d_0' % (n % 2), 'wd%d_1' % (n % 2)]
                for b2 in range(2):
                    tsl = slice(t0 + 512 * b2, t0 + 512 * b2 + 512)
                    bk = bank('B')
                    for f in range(22):
                        mm(PS(bk), wd[n % 2][:, f, :], actT[:, f, 512 * b2:512 * b2 + 512], f == 0, f == 21, wdk + ['actT'], [pk(bk)])
                    tt('vector', xT[:, n, tsl], PS(bk), xT[:, n, tsl], ALU.add, [pk(bk), 'xT'], ['xT'])
        S.barrier()
        if b_ == 0:
            dump('x2', xT[:, :, 0:256], 'xT')
        if stop_after == 'ffn':
            break

        cur[0] = phase_base
        for _ in norm_to_hT(2):
            pass
        pT = alloc([128, 2, S_LEN], BF16)
        pst = [alloc([128, 256], F32) for _ in range(2)]
        wpp = alloc([128, 2, 1024], BF16)
        wpg = [alloc([128, 8, 128], BF16) for _ in range(2)]
        sg = [alloc([128, 512], F32) for _ in range(2)]
        ptmp = [alloc([128, 512], F32) for _ in range(2)]
        ldc(wpp.rearrange('p a b -> p (a b)'), wpp_d.rearrange('p a b -> p (a b)'), 'wpp', ['wpp'])

        def load_wpg(n):
            ldc(wpg[n % 2].rearrange('p a b -> p (a b)'), wpg_d[n].rearrange('p a b -> p (a b)'), 'wpg%d' % (n % 2), ['wpg%d' % (n % 2)])
        load_wpg(0)
        for i in range(16):
            k = 'pst%d' % (i % 2)
            ld(pst[i % 2], p_d[b_, 128 * i:128 * i + 128, :], k, [k])
            bk = bank('A')
            for c in range(2):
                tr(PS(bk)[:, 128 * c:128 * c + 128], pst[i % 2][:, 128 * c:128 * c + 128], identf, [k, 'identf'], [pk(bk)])
            cp(evac_eng(), pT[:, :, 128 * i:128 * i + 128], PS(bk)[:, 0:256].rearrange('p (c t) -> p c t', c=2), [pk(bk)], ['pT'])
        cnt = 0
        for n in range(8):
            if n < 7:
                load_wpg(n + 1)
            wk = 'wpg%d' % (n % 2)
            for blk in range(4):
                tsl = slice(512 * blk, 512 * blk + 512)
                bk = bank('A')
                for kc in range(8):
                    mm(PS(bk), wpg[n % 2][:, kc, :], hT[:, kc, tsl], kc == 0, kc == 7, ['hT', wk], [pk(bk)])
                sgb = sg[cnt % 2]
                sgk = 'sg%d' % (cnt % 2)
                act(sgb, PS(bk), AF.Sigmoid, [pk(bk), 'bpg'], [sgk], scale=1.0, bias=bpg[:, n:n + 1])
                bk2 = bank('B')
                for k2 in range(2):
                    mm(PS(bk2), wpp[:, k2, 128 * n:128 * n + 128], pT[:, k2, tsl], k2 == 0, k2 == 1, ['wpp', 'pT'], [pk(bk2)])
                pt_ = ptmp[cnt % 2]
                ptk = 'ptmp%d' % (cnt % 2)
                tt('vector', pt_, PS(bk2), sgb, ALU.mult, [pk(bk2), sgk], [ptk])
                tt('vector', xT[:, n, tsl], xT[:, n, tsl], pt_, ALU.add, ['xT', ptk], ['xT'])
                cnt += 1
        S.barrier()
        if b_ == 0:
            dump('x3', xT[:, :, 0:256], 'xT')

        cur[0] = phase_base
        osb = [alloc([128, 1024], F32) for _ in range(2)]
        for blk, r, rk in norm_to_hT(None):
            tsl = slice(512 * blk, 512 * blk + 512)
            xk = 'xTf%d' % blk
            for c in range(8):
                stt(xT[:, c, tsl], xT[:, c, tsl], gains[:, 3, c:c + 1], r, ALU.mult, ALU.mult, ['gains', rk], [xk])
            for i4 in range(4):
                i = 4 * blk + i4
                ob = osb[i % 2]
                ok = 'osb%d' % (i % 2)
                for g in range(2):
                    bk = bank('B')
                    for c4 in range(4):
                        c = 4 * g + c4
                        tr(PS(bk)[:, 128 * c4:128 * c4 + 128], xT[:, c, 128 * i:128 * i + 128], identf, [xk, 'identf'], [pk(bk)])
                    cp(evac_eng(), ob[:, 512 * g:512 * g + 512], PS(bk), [pk(bk)], [ok])
                st(y_d[b_, 128 * i:128 * i + 128, :], ob, 'y%d' % (i % 2), [ok])
        if b_ == nseq - 1:
            S.barrier()
    S.barrier()
    S.emit()
    return nc


def _prep_shared(inp):
    f32 = np.float32
    bf = ml_dtypes.bfloat16
    W_in = np.asarray(inp['w_in'], f32)[0]

    def kc_layout(w):
        n = w.shape[1]
        return np.ascontiguousarray(w.reshape(8, 128, n).transpose(1, 0, 2))
    win_a = np.stack([kc_layout(np.concatenate([W_in[:, 128 * hp:128 * hp + 128], W_in[:, 512 + 128 * hp:512 + 128 * hp + 128],
                                                W_in[:, 1024 + 128 * hp:1024 + 128 * hp + 128]], axis=1)) for hp in range(4)])
    win_m = np.stack([kc_layout(np.concatenate([W_in[:, 1536 + 128 * h:1536 + 128 * h + 128], W_in[:, 2048 + 128 * h:2048 + 128 * h + 128],
                                                W_in[:, 2560 + 128 * h:2560 + 128 * h + 128]], axis=1)) for h in range(4)])
    win_g = kc_layout(W_in[:, 3072:3088])
    wmq = np.ascontiguousarray(np.asarray(inp['w_mq'], f32)[0].transpose(1, 0, 2))
    wmk = np.ascontiguousarray(np.asarray(inp['w_mk'], f32)[0].transpose(1, 0, 2))
    w_out = np.asarray(inp['w_out'], f32)[0]
    wout = np.stack([np.ascontiguousarray(w_out[512 * hf:512 * hf + 512].reshape(4, 128, 1024).transpose(1, 0, 2)) for hf in range(2)])
    wg_ = np.asarray(inp['w_ffn_gate'], f32)[0]
    wu_ = np.asarray(inp['w_ffn_up'], f32)[0]
    wgu = np.stack([kc_layout(np.concatenate([wg_[:, 128 * f:128 * f + 128], wu_[:, 128 * f:128 * f + 128]], axis=1)) for f in range(22)])
    wd_ = np.asarray(inp['w_ffn_down'], f32)[0]
    wfd = np.stack([np.ascontiguousarray(wd_[:, 128 * n:128 * n + 128].reshape(22, 128, 128).transpose(1, 0, 2)) for n in range(8)])
    wpg_ = np.asarray(inp['w_ple_gate'], f32)[0]
    wpg = np.stack([kc_layout(wpg_[:, 128 * n:128 * n + 128]) for n in range(8)])
    wpp_ = np.asarray(inp['w_ple_proj'], f32)[0]
    wpp = np.ascontiguousarray(wpp_.reshape(2, 128, 1024).transpose(1, 0, 2))

    def fm(v, n):
        return np.ascontiguousarray(np.asarray(v, f32).reshape(n, 128).T)
    gains = np.stack([fm(inp['ln_mix_g'][0], 8), fm(inp['ln_ffn_g'][0], 8), fm(inp['ln_ple_g'][0], 8), fm(inp['ln_final_g'], 8)], axis=1)
    bpg = fm(inp['b_ple_gate'][0], 8)
    fcw = np.ascontiguousarray(np.asarray(inp['ffn_conv_w'], f32)[0].reshape(3, 22, 128).transpose(2, 1, 0))
    fcb = fm(inp['ffn_conv_b'][0], 22)
    mcw = np.ascontiguousarray(np.asarray(inp['mlstm_conv_w'], f32)[0].reshape(3, 4, 128).transpose(2, 1, 0))
    mcb = fm(inp['mlstm_conv_b'][0], 4)
    gng = np.ascontiguousarray(np.broadcast_to(np.asarray(inp['mlstm_gn_g'], f32)[0][None, :], (128, 512)))
    gngT = fm(inp['mlstm_gn_g'][0], 4)
    big = np.asarray(inp['b_igate'], f32)[0]
    bfg = np.asarray(inp['b_fgate'], f32)[0]
    gb = np.concatenate([big[0], big[1], bfg[0], bfg[1]])
    gbias = np.ascontiguousarray(np.broadcast_to(gb[None, :], (128, 16)))
    inv = (500000.0 ** (-np.arange(0, 16, 2, dtype=np.float32) / np.float32(16))).astype(f32)
    invf = np.ascontiguousarray(np.broadcast_to(inv[None, :], (128, 8)))
    ki = np.arange(128)[:, None]
    col = np.arange(23 * 128)[None, :]
    d = (col // 128 - 11) * 128 + (col % 128) - ki
    ad = np.abs(d)
    c = (ad <= 64).astype(f32) + ((d % 4 == 0) & (ad <= 256)).astype(f32) + ((d % 16 == 0) & (ad <= 1024)).astype(f32)
    ss, tt_ = np.arange(128)[:, None], np.arange(128)[None, :]
    shared = dict(win_a=win_a, win_m=win_m, win_g=win_g, wmq=wmq, wmk=wmk, wout=wout, wgu=wgu, wfd=wfd, wpg=wpg, wpp=wpp,
                  gains=np.ascontiguousarray(gains), bpg=bpg, fcw=fcw, fcb=fcb, mcw=mcw, mcb=mcb, gng=gng, gngT=gngT, gbias=gbias, invf=invf,
                  identb=np.eye(128).astype(bf), identf=np.eye(128, dtype=f32), maskc=c.astype(bf),
                  masku=(ss <= tt_).astype(bf), maskl=(ss >= tt_).astype(bf),
                  triu=(ss <= tt_).astype(f32), tril=(ss >= tt_).astype(f32))
    return shared


def _core_maps(inp, shared, ncores=8):
    x = np.asarray(inp['x'], np.float32)
    p = np.asarray(inp['p'], np.float32)[0]
    pos = np.asarray(inp['positions'], np.int32)
    maps = []
    for c in range(ncores):
        m = dict(shared)
        m['x'] = np.ascontiguousarray(x[2 * c:2 * c + 2])
        m['p'] = np.ascontiguousarray(p[2 * c:2 * c + 2])
        m['pos'] = np.ascontiguousarray(pos[2 * c:2 * c + 2].reshape(2, 16, 128).transpose(0, 2, 1))
        maps.append(m)
    return maps


def kernel(**inputs):
    shared = _prep_shared(inputs)
    maps = _core_maps(inputs, shared)
    nc = build_program()
    res = run_bass_kernel_spmd(nc, maps, core_ids=list(range(8)))
    out = np.concatenate([np.asarray(r['y'], np.float32) for r in res.results], axis=0)
    return out
```

```python
import math
from contextlib import ExitStack
import numpy as np
import ml_dtypes
import concourse.bass as bass
import concourse.mybir as mybir
from concourse.bass_utils import run_bass_kernel_spmd

F32 = mybir.dt.float32
BF16 = mybir.dt.bfloat16
I32 = mybir.dt.int32
U8 = mybir.dt.uint8
AF = mybir.ActivationFunctionType
ALU = mybir.AluOpType

ENG_NAMES = ['sync', 'scalar', 'vector', 'gpsimd', 'tensor']
S_LEN = 2048
NSEQ = 2
EPS = 1e-6


class Sched:
    def __init__(self, nc):
        self.nc = nc
        self.ops = {e: [] for e in ENG_NAMES}
        self.nops = {e: 0 for e in ENG_NAMES}
        self.last_w = {}
        self.readers = {}
        self.waited = {e: {} for e in ENG_NAMES}
        self.dma_cnt = {}
        self.signal = {e: set() for e in ENG_NAMES}

    def _need(self, eng, tok, raw=False):
        if tok is None:
            return
        kind, src, idx = tok
        if kind == 'eng' and src == eng:
            if not raw or eng == 'tensor' or eng == 'sync':
                return
        key = (kind, src)
        if self.waited[eng].get(key, 0) >= idx:
            return
        self.waited[eng][key] = idx
        self.ops[eng].append(('wait', tok))
        if kind == 'eng':
            self.signal[src].add(idx)

    def _deps(self, eng, reads, writes):
        for k in reads:
            self._need(eng, self.last_w.get(k), raw=True)
        for k in writes:
            self._need(eng, self.last_w.get(k))
            for t in self.readers.get(k, ()):
                self._need(eng, t)

    def _commit(self, tok, reads, writes):
        for k in reads:
            self.readers.setdefault(k, []).append(tok)
        for k in writes:
            self.last_w[k] = tok
            self.readers[k] = []

    def op(self, eng, fn, reads=(), writes=()):
        self._deps(eng, reads, writes)
        self.nops[eng] += 1
        idx = self.nops[eng]
        self.ops[eng].append(('op', fn, idx))
        tok = ('eng', eng, idx)
        self._commit(tok, reads, writes)
        return tok

    def dma(self, eng, fn, stream, reads=(), writes=()):
        self._deps(eng, reads, writes)
        cnt = self.dma_cnt.get(stream, 0) + 1
        self.dma_cnt[stream] = cnt
        self.ops[eng].append(('dma', fn, stream))
        tok = ('dma', stream, cnt)
        self._commit(tok, reads, writes)
        return tok

    def wait_all(self, eng, keys):
        for k in keys:
            self._need(eng, self.last_w.get(k))
            for t in self.readers.get(k, ()):
                self._need(eng, t)

    def barrier(self):
        toks = [('eng', e, self.nops[e]) for e in ENG_NAMES if self.nops[e] > 0]
        toks += [('dma', s, c) for s, c in self.dma_cnt.items()]
        for e in ENG_NAMES:
            for t in toks:
                self._need(e, t)

    def emit(self):
        nc = self.nc
        with ExitStack() as st:
            sems = {}
            for e in ENG_NAMES:
                sems[('eng', e)] = st.enter_context(nc.semaphore('s_' + e))
            for s in self.dma_cnt:
                sems[('dma', s)] = st.enter_context(nc.semaphore('d_' + s))
            rank = {}
            for e in ENG_NAMES:
                for r, idx in enumerate(sorted(self.signal[e])):
                    rank[(e, idx)] = r + 1
            block = st.enter_context(nc.Block())

            def make(e):
                def body(engh):
                    for item in self.ops[e]:
                        if item[0] == 'wait':
                            kind, src, idx = item[1]
                            if kind == 'eng':
                                engh.wait_ge(sems[('eng', src)], rank[(src, idx)])
                            else:
                                engh.wait_ge(sems[('dma', src)], 16 * idx)
                        elif item[0] == 'op':
                            ins = item[1](engh)
                            if (e, item[2]) in rank:
                                ins.then_inc(sems[('eng', e)], 1)
                        else:
                            ins = item[1](engh)
                            ins.then_inc(sems[('dma', item[2])], 16)
                return body
            for e in ENG_NAMES:
                getattr(block, e)(make(e))


def build_program(dbg=None, nseq=NSEQ, stop_after=None):
    nc = bass.Bass("TRN2", target_bir_lowering=False)
    S = Sched(nc)

    def din(name, shape, dt=F32):
        return nc.dram_tensor(name, list(shape), dt, kind="ExternalInput").ap()

    x_d = din("x", [NSEQ, S_LEN, 1024])
    p_d = din("p", [NSEQ, S_LEN, 256])
    pos_d = din("pos", [NSEQ, 128, 16], I32)
    wina_d = din("win_a", [4, 128, 8, 384])
    winm_d = din("win_m", [4, 128, 8, 384])
    wing_d = din("win_g", [128, 8, 16])
    wmq_d = din("wmq", [128, 4, 128])
    wmk_d = din("wmk", [128, 4, 128])
    wout_d = din("wout", [2, 128, 4, 1024])
    wgu_d = din("wgu", [22, 128, 8, 256])
    wfd_d = din("wfd", [8, 128, 22, 128])
    wpg_d = din("wpg", [8, 128, 8, 128])
    wpp_d = din("wpp", [128, 2, 1024])
    gains_d = din("gains", [128, 4, 8])
    bpg_d = din("bpg", [128, 8])
    fcw_d = din("fcw", [128, 22, 3])
    fcb_d = din("fcb", [128, 22])
    mcw_d = din("mcw", [128, 4, 3])
    mcb_d = din("mcb", [128, 4])
    gng_d = din("gng", [128, 512])
    gngT_d = din("gngT", [128, 4])
    gbias_d = din("gbias", [128, 16])
    invf_d = din("invf", [128, 8])
    identb_d = din("identb", [128, 128], BF16)
    identf_d = din("identf", [128, 128])
    maskc_d = din("maskc", [128, 23 * 128], BF16)
    masku_d = din("masku", [128, 128], BF16)
    maskl_d = din("maskl", [128, 128], BF16)
    triu_d = din("triu", [128, 128])
    tril_d = din("tril", [128, 128])
    y_d = nc.dram_tensor("y", [NSEQ, S_LEN, 1024], F32, kind="ExternalOutput").ap()
    scr_d = nc.dram_tensor("rden_scr", [8, 512], F32).ap()
    scr2_d = nc.dram_tensor("rden_scr2", [8, 512], F32).ap()
    dbg_d = {}
    if dbg:
        for name, shape in dbg.items():
            dbg_d[name] = nc.dram_tensor("dbg_" + name, list(shape), F32, kind="ExternalOutput").ap()

    ARENA = 207 * 1024
    arena = nc.alloc_sbuf_tensor("arena", [128, ARENA], U8)
    cur = [0]
    marks = {}

    def alloc(shape, dt, name=None):
        esz = {F32: 4, BF16: 2, I32: 4}[dt]
        n = int(np.prod(shape[1:]))
        nb = n * esz
        off = (cur[0] + 63) // 64 * 64
        assert off + nb <= ARENA, (name, off, nb, ARENA)
        cur[0] = off + nb
        v = arena[:, off:off + nb].bitcast(dt)
        if len(shape) == 3:
            v = v.rearrange('p (a b) -> p a b', a=shape[1])
        elif len(shape) == 4:
            v = v.rearrange('p (a b c) -> p a b c', a=shape[1], b=shape[2])
        return v

    psb = [nc.alloc_psum_tensor("ps%d" % i, [128, 512], F32) for i in range(8)]
    rot = {'A': [0, 1, 2, 3], 'B': [4, 5, 6, 7], 'S': [0, 1, 2, 3, 6, 7]}
    rotc = {'A': 0, 'B': 0, 'S': 0}

    def bank(group='A'):
        i = rot[group][rotc[group] % len(rot[group])]
        rotc[group] += 1
        return i

    def PS(i):
        return psb[i][:]

    def PSB(i):
        return psb[i][:].bitcast(BF16)

    def pk(i):
        return 'ps%d' % i

    def act(out, in_, func, reads, writes, **kw):
        return S.op('scalar', lambda e: e.activation(out=out, in_=in_, func=func, **kw), reads, writes)

    def mm(out, lhsT, rhs, start, stop, reads, writes):
        return S.op('tensor', lambda e: e.matmul(out, lhsT=lhsT, rhs=rhs, start=start, stop=stop), reads, writes)

    def tr(out, in_, ident, reads, writes):
        return S.op('tensor', lambda e: e.transpose(out=out, in_=in_, identity=ident), reads, writes)

    def tt(eng, out, in0, in1, op, reads, writes):
        return S.op(eng, lambda e: e.tensor_tensor(out=out, in0=in0, in1=in1, op=op), reads, writes)

    def ts(eng, out, in0, s1, s2, op0, op1, reads, writes):
        if op1 is None:
            return S.op(eng, lambda e: e.tensor_scalar(out=out, in0=in0, scalar1=s1, scalar2=None, op0=op0), reads, writes)
        return S.op(eng, lambda e: e.tensor_scalar(out=out, in0=in0, scalar1=s1, scalar2=s2, op0=op0, op1=op1), reads, writes)

    def stt(out, in0, scalar, in1, op0, op1, reads, writes):
        return S.op('vector', lambda e: e.scalar_tensor_tensor(out=out, in0=in0, scalar=scalar, in1=in1, op0=op0, op1=op1), reads, writes)

    def cp(eng, out, in_, reads, writes):
        if eng == 'scalar':
            return S.op(eng, lambda e: e.copy(out=out, in_=in_), reads, writes)
        return S.op(eng, lambda e: e.tensor_copy(out=out, in_=in_), reads, writes)

    def recip(out, in_, reads, writes):
        return S.op('vector', lambda e: e.reciprocal(out=out, in_=in_), reads, writes)

    def bnstats(out, in_, reads, writes):
        return S.op('vector', lambda e: e.bn_stats(out=out, in_=in_), reads, writes)

    def bnaggr(out, in_, reads, writes):
        return S.op('vector', lambda e: e.bn_aggr(out=out, in_=in_), reads, writes)

    def memset(eng, ap, val, writes):
        return S.op(eng, lambda e: e.memset(ap, val), (), writes)

    def ld(out, in_, stream, writes, reads=()):
        return S.dma('sync', lambda e: e.dma_start(out=out, in_=in_), stream, reads, writes)

    def ldc(out, in_, stream, writes, reads=()):
        return S.dma('gpsimd', lambda e: e.dma_start(out=out, in_=in_), stream, reads, writes)

    def st(out, in_, stream, reads):
        return S.dma('sync', lambda e: e.dma_start(out=out, in_=in_), stream, reads, ())

    def dump(name, ap, key):
        if name in dbg_d:
            S.dma('gpsimd', lambda e: e.dma_start(out=dbg_d[name], in_=ap), 'dbg_' + name, [key], ())

    xT = alloc([128, 8, S_LEN], F32)
    hT = alloc([128, 8, S_LEN], BF16)
    identb = alloc([128, 128], BF16)
    identf = alloc([128, 128], F32)
    maskc = alloc([128, 23 * 128], BF16)
    masku = alloc([128, 128], BF16)
    maskl = alloc([128, 128], BF16)
    triu = alloc([128, 128], F32)
    tril = alloc([128, 128], F32)
    onesf = alloc([128, 128], F32)
    onesb = alloc([128, 128], BF16)
    gains = alloc([128, 4, 8], F32)
    bpg = alloc([128, 8], F32)
    fcw = alloc([128, 22, 3], F32)
    fcb = alloc([128, 22], F32)
    mcw = alloc([128, 4, 3], F32)
    mcb = alloc([128, 4], F32)
    gngT = alloc([128, 4], F32)
    gbias = alloc([128, 16], F32)
    invf = alloc([128, 8], F32)
    epsc = alloc([128, 1], F32)
    onec = alloc([128, 1], F32)
    sq = [alloc([128, 512], BF16) for _ in range(2)]
    rs = [alloc([128, 512], F32) for _ in range(2)]
    wmq = alloc([128, 4, 128], BF16)
    wmk = alloc([128, 4, 128], BF16)
    phase_base = cur[0]

    consts = [(identb, identb_d, 'identb'), (identf, identf_d, 'identf'), (maskc, maskc_d, 'maskc'),
              (masku, masku_d, 'masku'), (maskl, maskl_d, 'maskl'), (triu, triu_d, 'triu'), (tril, tril_d, 'tril'),
              (gains, gains_d, 'gains'), (bpg, bpg_d, 'bpg'), (fcw, fcw_d, 'fcw'), (fcb, fcb_d, 'fcb'),
              (mcw, mcw_d, 'mcw'), (mcb, mcb_d, 'mcb'), (gngT, gngT_d, 'gngT'), (gbias, gbias_d, 'gbias'),
              (invf, invf_d, 'invf')]
    for sb, d, nm in consts:
        ld(sb, d, 'c_' + nm, [nm])
    ldc(wmq, wmq_d, 'c_wmq', ['wmq'])
    ldc(wmk, wmk_d, 'c_wmk', ['wmk'])
    memset('vector', onesf, 1.0, ['onesf'])
    memset('vector', onesb, 1.0, ['onesb'])
    memset('vector', epsc, EPS, ['epsc'])
    memset('vector', onec, 1.0, ['onec'])

    evac_rr = [0]

    def evac_eng():
        evac_rr[0] += 1
        return 'scalar' if evac_rr[0] % 2 else 'vector'

    def norm_to_hT(gi):
        for blk in range(4):
            tsl = slice(512 * blk, 512 * blk + 512)
            b = bank('A')
            for c in range(8):
                sqb = sq[c % 2]
                act(sqb, xT[:, c, tsl], AF.Square, ['xT'], ['sq%d' % (c % 2)])
                mm(PS(b), onesb, sqb, c == 0, c == 7, ['onesb', 'sq%d' % (c % 2)], [pk(b)])
            r = rs[blk % 2]
            rk = 'rs%d' % (blk % 2)
            act(r, PS(b), AF.Ln, [pk(b), 'epsc'], [rk], scale=1.0 / 1024.0, bias=epsc)
            act(r, r, AF.Exp, [rk], [rk], scale=-0.5)
            for c in range(8):
                if gi is None:
                    continue
                stt(hT[:, c, tsl], xT[:, c, tsl], gains[:, gi, c:c + 1], r, ALU.mult, ALU.mult,
                    ['xT', 'gains', rk], ['hT'])
            yield blk, r, rk

    for b_ in range(nseq):
        if stop_after == 'consts':
            dump('xT', onesf[:, 0:8].unsqueeze(1).broadcast_to([128, 8, 8]) if False else xT[:, :, 0:256], 'onesf')
            break
        cur[0] = phase_base
        if b_ > 0:
            cur[0] = phase_base + 2 * 4096
        NXS = 6
        xs = [alloc([128, 1024], F32) for _ in range(NXS)]
        for i in range(16):
            xsb = xs[i % NXS]
            k = 'xs%d' % (i % NXS)
            if b_ > 0:
                S.dma('gpsimd', lambda e, o_=xsb, i_=x_d[b_, 128 * i:128 * i + 128, :]: e.dma_start(out=o_, in_=i_), k, (), [k])
            else:
                ld(xsb, x_d[b_, 128 * i:128 * i + 128, :], k, [k])
            for g in range(2):
                bk = bank('A')
                for c4 in range(4):
                    c = 4 * g + c4
                    tr(PS(bk)[:, 128 * c4:128 * c4 + 128], xsb[:, 128 * c:128 * c + 128], identf, [k, 'identf'], [pk(bk)])
                cp(evac_eng(), xT[:, 4 * g:4 * g + 4, 128 * i:128 * i + 128],
                   PS(bk).rearrange('p (c t) -> p c t', c=4), [pk(bk)], ['xT', 'xTf%d' % (i // 4)])
        S.barrier()
        cur[0] = phase_base
        if b_ == 0:
            dump('xT', xT[:, :, 0:256], 'xT')

        if stop_after == 'load':
            break
        for _ in norm_to_hT(0):
            pass
        if stop_after == 'norm':
            break
        mixT = alloc([128, 4, S_LEN], BF16)
        wu = [alloc([128, 8, 384], BF16) for _ in range(2)]
        unit_base = cur[0]

        def load_wu(u):
            slot = u % 2
            src = wina_d[u] if u < 4 else winm_d[u - 4]
            for hh in range(2):
                ldc(wu[slot][:, 4 * hh:4 * hh + 4, :].rearrange('p a b -> p (a b)'),
                    src[:, 4 * hh:4 * hh + 4, :].rearrange('p a b -> p (a b)'),
                    'wu%d_%d' % (slot, hh), ['wu%d_%d' % (slot, hh)])

        def load_wo(half):
            for hh in range(2):
                ldc(wo[:, 2 * hh:2 * hh + 2, :].rearrange('p a b -> p (a b)'),
                    wout_d[half, :, 2 * hh:2 * hh + 2, :].rearrange('p a b -> p (a b)'),
                    'wo_%d' % hh, ['wo_%d' % hh])

        def apply_wout():
            for n in range(8):
                for blk in range(4):
                    tsl = slice(512 * blk, 512 * blk + 512)
                    bk = bank('B')
                    for u in range(4):
                        mm(PS(bk), wo[:, u, 128 * n:128 * n + 128], mixT[:, u, tsl], u == 0, u == 3,
                           ['wo_0', 'wo_1', 'mixT'], [pk(bk)])
                    tt('vector', xT[:, n, tsl], PS(bk), xT[:, n, tsl], ALU.add, [pk(bk), 'xT'], ['xT'])

        load_wu(0)

        posi = alloc([128, 16], I32)
        posf = alloc([128, 16], F32)
        ang = alloc([128, 16, 8], F32)
        rt1 = alloc([128, 16, 8], F32)
        rt2 = alloc([128, 16, 8], F32)
        cosT = alloc([128, 16, 8], F32)
        sinT = alloc([128, 16, 8], F32)
        ld(posi, pos_d[b_], 'posi', ['posi'])
        cp('vector', posf, posi, ['posi'], ['posf'])
        for j in range(8):
            ts('vector', ang[:, :, j], posf, invf[:, j:j + 1], None, ALU.mult, None, ['posf', 'invf'], ['ang'])
        TWO_PI = 2.0 * math.pi
        MAGIC = 12582912.0
        for which, shift, dst, dk in ((0, 0.0, sinT, 'sinT'), (1, math.pi / 2, cosT, 'cosT')):
            ts('vector', rt1, ang, 1.0 / TWO_PI, shift / TWO_PI, ALU.mult, ALU.add, ['ang'], ['rt1'])
            ts('vector', rt1, rt1, MAGIC, None, ALU.add, None, ['rt1'], ['rt1'])
            ts('vector', rt1, rt1, MAGIC, None, ALU.subtract, None, ['rt1'], ['rt1'])
            ts('vector', rt2, ang, shift, None, ALU.add, None, ['ang'], ['rt2'])
            stt(rt2, rt1, -TWO_PI, rt2, ALU.mult, ALU.add, ['rt1', 'rt2'], ['rt2'])
            ts('vector', rt2, rt2, 3.14159, -3.14159, ALU.min, ALU.max, ['rt2'], ['rt2'])
            act(dst, rt2, AF.Sin, ['rt2'], [dk])
        att_base = cur[0]
        if b_ == 0:
            dump('cosT', cosT, 'cosT')
            dump('sinT', sinT, 'sinT')
            dump('ang', ang, 'ang')
        if stop_after == 'rot':
            break

        att_end = [0]
        deferred = []
        DEFER1 = 16
        DEFER2 = 16

        def tick_deferred(flush=False):
            for item in list(deferred):
                item[0] -= 1
                if item[0] <= 0 or flush:
                    item[2](*item[1])
                    deferred.remove(item)

        for hp in range(4):
            if hp == 1:
                cur[0] = att_end[0]
                wo = alloc([128, 4, 1024], BF16)
                load_wo(0)
            cur[0] = att_base
            slot = hp % 2
            wuk = ['wu%d_0' % slot, 'wu%d_1' % slot]
            load_wu(hp + 1)
            QTz = [alloc([128, S_LEN], BF16) for _ in range(2)]
            KT = alloc([128, S_LEN], BF16)
            V1 = alloc([128, 16, 2, 128], BF16)
            rdf = alloc([128, 512], F32)
            Rs = [alloc([128, 512], F32) for _ in range(4)]
            dcol = [alloc([128, 4], F32) for _ in range(2)]
            qk = [alloc([128, 4, 64], F32) for _ in range(2)]
            ra = alloc([128, 4, 8], F32)
            rb = alloc([128, 4, 8], F32)
            ra2 = alloc([128, 4, 8], F32)
            rb2 = alloc([128, 4, 8], F32)
            Eb = [alloc([128, 512], BF16) for _ in range(6)]
            Pb = [alloc([128, 512], BF16) for _ in range(6)]
            att_end[0] = max(att_end[0], cur[0])
            memset('gpsimd', V1, 0.0, ['V1'])
            memset('gpsimd', V1[:, :, 0, 64:65], 1.0, ['V1'])
            memset('gpsimd', V1[:, :, 1, 0:1], 1.0, ['V1'])
            memset('gpsimd', QTz[0][64:128, :], 0.0, ['QT'])
            memset('gpsimd', QTz[1][0:64, :], 0.0, ['QT'])

            def proj_front(i):
                tsl = slice(128 * i, 128 * i + 128)
                zb = bank('A')
                for kc in range(8):
                    mm(PS(zb)[:, 0:384], hT[:, kc, tsl], wu[slot][:, kc, :], kc == 0, kc == 7, ['hT'] + wuk, [pk(zb)])
                z4 = PS(zb)[:, 0:256].rearrange('p (a b) -> p a b', a=4)
                qkb = qk[i % 2]
                qkk = 'qk%d' % (i % 2)
                cb = cosT[:, i:i + 1, :].broadcast_to([128, 4, 8])
                sb_ = sinT[:, i:i + 1, :].broadcast_to([128, 4, 8])
                t1 = z4[:, :, 0:8]
                t2 = z4[:, :, 8:16]
                tt('vector', ra, t1, cb, ALU.mult, [pk(zb), 'cosT'], ['ra'])
                tt('vector', rb, t2, sb_, ALU.mult, [pk(zb), 'sinT'], ['rb'])
                tt('vector', ra2, t2, cb, ALU.mult, [pk(zb), 'cosT'], ['ra2'])
                tt('vector', rb2, t1, sb_, ALU.mult, [pk(zb), 'sinT'], ['rb2'])
                tt('vector', qkb[:, :, 0:8], ra, rb, ALU.subtract, ['ra', 'rb'], [qkk])
                tt('vector', qkb[:, :, 8:16], ra2, rb2, ALU.add, ['ra2', 'rb2'], [qkk])
                cp('scalar', qkb[:, :, 16:64], z4[:, :, 16:64], [pk(zb)], [qkk])
                cp('scalar', V1[:, i, 0, 0:64], PS(zb)[:, 256:320], [pk(zb)], ['V1'])
                cp('scalar', V1[:, i, 1, 64:128], PS(zb)[:, 320:384], [pk(zb)], ['V1'])

            def proj_back(i):
                tsl = slice(128 * i, 128 * i + 128)
                qkb = qk[i % 2]
                qkk = 'qk%d' % (i % 2)
                qk2 = qkb.rearrange('p a b -> p (a b)')
                tb = bank('A')
                tb2 = bank('A')
                tr(PS(tb)[:, 0:128], qk2[:, 0:128], identf, [qkk, 'identf'], [pk(tb)])
                tr(PS(tb2)[:, 0:128], qk2[:, 128:256], identf, [qkk, 'identf'], [pk(tb2)])
                cp('vector', QTz[0][0:64, tsl], PS(tb)[0:64, 0:128], [pk(tb)], ['QT'])
                cp('vector', QTz[1][64:128, tsl], PS(tb)[64:128, 0:128], [pk(tb)], ['QT'])
                cp('scalar', KT[:, tsl], PS(tb2)[:, 0:128], [pk(tb2)], ['KT'])

            for i in range(17):
                if i < 16:
                    proj_front(i)
                if i >= 1:
                    proj_back(i - 1)
                if i >= 3:
                    tick_deferred()
                    tick_deferred()
            if stop_after == 'proj':
                break
            if b_ == 0 and hp == 0:
                dump('KT', KT[:, 0:512], 'KT')
            accb = [4, 5, 6, 7]
            units = []
            gi = 0
            for QC in range(4):
                for hh in range(2):
                    kbs = [kb for kb in range(16) if any(abs(4 * QC + j - kb) <= 8 for j in range(4))]
                    for kb in kbs:
                        units.append((QC, hh, kb, kb == kbs[0], kb == kbs[-1], gi))
                    gi += 1
            NBUF = len(Eb)
            LOOK = 4

            def vcols(QC, kb):
                js = [j for j in range(4) if abs(4 * QC + j - kb) <= 8]
                return 128 * js[0], 128 * (js[-1] + 1)

            def front(n):
                QC, hh, kb, first, last, g = units[n]
                c0, c1 = vcols(QC, kb)
                sb2 = bank('A')
                mm(PS(sb2)[:, c0:c1], KT[:, 128 * kb:128 * kb + 128], QTz[hh][:, 512 * QC + c0:512 * QC + c1], True, True,
                   ['KT', 'QT'], [pk(sb2)])
                e_ = Eb[n % NBUF]
                ek = 'E%d' % (n % NBUF)
                p_ = Pb[n % NBUF]
                pk_ = 'P%d' % (n % NBUF)
                act(e_[:, c0:c1], PS(sb2)[:, c0:c1], AF.Exp, [pk(sb2)], [ek], scale=0.125)
                d0 = 4 * QC - kb + 11
                tt('vector', p_[:, c0:c1], e_[:, c0:c1], maskc[:, 128 * d0 + c0:128 * d0 + c1], ALU.mult, [ek, 'maskc'], [pk_])

            def back(n):
                QC, hh, kb, first, last, g = units[n]
                qsl = slice(512 * QC, 512 * QC + 512)
                p_ = Pb[n % NBUF]
                pk_ = 'P%d' % (n % NBUF)
                ab_ = accb[g % 4]
                c0, c1 = vcols(QC, kb)
                if hh == 0:
                    mm(PS(ab_)[0:65, c0:c1], V1[:, kb, 0, 0:65], p_[:, c0:c1], first, last, [pk_, 'V1'], [pk(ab_)])
                else:
                    mm(PS(ab_)[:, c0:c1], V1[:, kb, 1, :], p_[:, c0:c1], first, last, [pk_, 'V1'], [pk(ab_)])
                if last:
                    pr = 64 if hh == 0 else 0
                    rows = slice(0, 64) if hh == 0 else slice(64, 128)
                    prs = slice(pr, pr + 1)
                    cp('vector', rdf[prs, :], PS(ab_)[prs, :], [pk(ab_)], ['rdf'])
                    g8 = g % 8
                    dc_ = dcol[g % 2]
                    dck = 'dcol%d' % (g % 2)
                    rs_ = Rs[g % 4]
                    rsk = 'Rs%d' % (g % 4)
                    S.dma('sync', lambda e, o_=scr_d[g8:g8 + 1, :], i_=rdf[prs, :]: e.dma_start(out=o_, in_=i_),
                          'scrw%d' % g8, ['rdf'], ['scrA%d' % g8])
                    S.dma('sync', lambda e, o_=dc_, i_=scr_d[g8, :].rearrange('(p f) -> p f', f=4): e.dma_start(out=o_, in_=i_),
                          'dcr%d' % (g % 2), ['scrA%d' % g8], [dck])
                    deferred.append([DEFER1, (hp, hh, g, qsl, ab_, prs, rows), finalize_s1])

            def finalize_s1(hp_, hh, g, qsl, ab_, prs, rows, Rs=Rs, dcol=dcol):
                g8 = g % 8
                dc_ = dcol[g % 2]
                dck = 'dcol%d' % (g % 2)
                rs_ = Rs[g % 4]
                rsk = 'Rs%d' % (g % 4)
                recip(dc_, dc_, [dck], [dck])
                S.dma('sync', lambda e, o_=scr2_d[g8, :].rearrange('(p f) -> p f', f=4), i_=dc_: e.dma_start(out=o_, in_=i_),
                      'dcw%d' % g8, [dck], ['scrB%d' % g8])
                S.dma('sync', lambda e, o_=rs_[rows, :], i_=scr2_d[g8:g8 + 1, :].broadcast_to([64, 512]): e.dma_start(out=o_, in_=i_),
                      'scrr%d' % (g % 4), ['scrB%d' % g8], [rsk])
                deferred.append([DEFER2, (hp_, hh, g, qsl, ab_, prs, rows), finalize_pe])

            def finalize_pe(hp_, hh, g, qsl, ab_, prs, rows, Rs=Rs):
                rs_ = Rs[g % 4]
                rsk = 'Rs%d' % (g % 4)
                tt('vector', mixT[rows, hp_, qsl], PS(ab_)[rows, :], rs_[rows, :], ALU.mult, [pk(ab_), rsk], ['mixT'])

            for n in range(len(units) + LOOK):
                if n < len(units):
                    front(n)
                if n >= LOOK:
                    back(n - LOOK)
                tick_deferred()
        if stop_after == 'proj':
            break
        while deferred:
            tick_deferred(flush=True)
        if b_ == 0:
            dump('mixA', mixT[:, :, 0:256], 'mixT')
        apply_wout()
        S.barrier()
        if stop_after == 'attn':
            break

        cur[0] = unit_base
        wg = alloc([128, 8, 16], BF16)
        gates = alloc([128, 16, 16], F32)
        e1 = alloc([128, 16, 8], F32)
        lf = [alloc([128, 64], F32) for _ in range(2)]
        igs = [alloc([128, 64], F32) for _ in range(2)]
        ea = [alloc([128, 64], F32) for _ in range(2)]
        eb = [alloc([128, 64], F32) for _ in range(2)]
        eg = [alloc([128, 64], F32) for _ in range(2)]
        ldc(wg.rearrange('p a b -> p (a b)'), wing_d.rearrange('p a b -> p (a b)'), 'wg', ['wg'])
        gb = bank('A')
        for i in range(16):
            for kc in range(8):
                mm(PS(gb)[:, 16 * i:16 * i + 16], hT[:, kc, 128 * i:128 * i + 128], wg[:, kc, :], kc == 0, kc == 7,
                   ['hT', 'wg'], [pk(gb)])
        tt('vector', gates, PS(gb)[:, 0:256].rearrange('p (a b) -> p a b', a=16),
           gbias.unsqueeze(1).broadcast_to([128, 16, 16]), ALU.add, [pk(gb), 'gbias'], ['gates'])
        act(e1, gates[:, :, 8:16], AF.Exp, ['gates'], ['e1'], scale=-1.0)
        for d in range(2):
            act(lf[d].rearrange('p (a b) -> p a b', a=16), e1[:, :, 4 * d:4 * d + 4], AF.Ln, ['e1', 'onec'], ['lf%d' % d],
                scale=1.0, bias=onec)
        cb_ = bank('A')
        tri = [triu, tril]
        for d in range(2):
            mm(PS(cb_)[:, 64 * d:64 * d + 64], tri[d], lf[d], True, True, ['triu', 'tril', 'lf%d' % d], [pk(cb_)])
        for d in range(2):
            mm(PS(cb_)[:, 128 + 64 * d:128 + 64 * d + 64], onesf, lf[d], True, True, ['onesf', 'lf%d' % d], [pk(cb_)])
        for d in range(2):
            tt('vector', igs[d].rearrange('p (a b) -> p a b', a=16), gates[:, :, 4 * d:4 * d + 4],
               PS(cb_)[:, 64 * d:64 * d + 64].rearrange('p (a b) -> p a b', a=16), ALU.add, ['gates', pk(cb_)], ['igs%d' % d])
            act(ea[d], igs[d], AF.Exp, ['igs%d' % d], ['ea%d' % d])
            act(eb[d], PS(cb_)[:, 64 * d:64 * d + 64], AF.Exp, [pk(cb_)], ['eb%d' % d], scale=-1.0)
            act(eg[d], PS(cb_)[:, 128 + 64 * d:128 + 64 * d + 64], AF.Exp, [pk(cb_)], ['eg%d' % d], scale=-1.0)
        munit_base = cur[0]
        if b_ == 0:
            dump('gates', gates, 'gates')
            for d_ in range(2):
                dump('lf%d' % d_, lf[d_], 'lf%d' % d_)
                dump('ea%d' % d_, ea[d_], 'ea%d' % d_)
                dump('eb%d' % d_, eb[d_], 'eb%d' % d_)
                dump('eg%d' % d_, eg[d_], 'eg%d' % d_)

        for h in range(4):
            cur[0] = munit_base
            u = 4 + h
            slot = u % 2
            wuk = ['wu%d_0' % slot, 'wu%d_1' % slot]
            if h < 3:
                load_wu(u + 1)
            u_sb = alloc([128, S_LEN + 2], F32)
            ctmp = [alloc([128, 512], F32) for _ in range(2)]
            sgt = [alloc([128, 512], BF16) for _ in range(2)]
            ucT = alloc([128, S_LEN], BF16)
            qT = alloc([128, S_LEN], BF16)
            kT = alloc([128, S_LEN], BF16)
            ktok = alloc([128, 16, 128], BF16)
            Vp = [alloc([128, 16, 129], BF16) for _ in range(2)]
            sigo = alloc([128, 16, 128], BF16)
            Cbf = [alloc([128, 16, 129], BF16) for _ in range(2)]
            Y = [alloc([128, 129], F32) for _ in range(2)]
            PTm = [alloc([128, 128], BF16) for _ in range(4)]
            dn = alloc([128, 32], F32)
            dneg = alloc([128, 32], F32)
            rr = alloc([128, 32], F32)
            s1 = alloc([128, 16], F32)
            s2 = alloc([128, 16], F32)
            rstd = alloc([128, 16], F32)
            hn = [alloc([128, 128], F32) for _ in range(2)]
            mem = [alloc([128, 128], F32) for _ in range(2)]
            memset('vector', u_sb[:, 0:1], 0.0, ['u_sb'])
            memset('vector', u_sb[:, S_LEN + 1:S_LEN + 2], 0.0, ['u_sb'])
            for blk in range(4):
                tsl = slice(512 * blk, 512 * blk + 512)
                bk = bank('A')
                for kc in range(8):
                    mm(PS(bk), wu[slot][:, kc, 0:128], hT[:, kc, tsl], kc == 0, kc == 7, ['hT'] + wuk, [pk(bk)])
                cp('scalar', u_sb[:, 1 + 512 * blk:1 + 512 * blk + 512], PS(bk), [pk(bk)], ['u_sb'])
            def conv_op(k):
                blk, step = divmod(k, 3)
                o = 512 * blk
                ct = ctmp[blk % 2]
                ck = 'ct%d' % (blk % 2)
                if step == 0:
                    ts('vector', ct, u_sb[:, 1 + o:1 + o + 512], mcw[:, h, 1:2], mcb[:, h:h + 1], ALU.mult, ALU.add,
                       ['u_sb', 'mcw', 'mcb'], [ck])
                elif step == 1:
                    stt(ct, u_sb[:, o:o + 512], mcw[:, h, 0:1], ct, ALU.mult, ALU.add, ['u_sb', 'mcw', ck], [ck])
                else:
                    stt(ct, u_sb[:, 2 + o:2 + o + 512], mcw[:, h, 2:3], ct, ALU.mult, ALU.add, ['u_sb', 'mcw', ck], [ck])
                    sg_ = sgt[blk % 2]
                    sgk_ = 'sgt%d' % (blk % 2)
                    act(sg_, ct, AF.Sigmoid, [ck], [sgk_])
                    tt('vector', ucT[:, o:o + 512], ct, sg_, ALU.mult, [ck, sgk_], ['ucT'])

            def qk_block(blk):
                tsl = slice(512 * blk, 512 * blk + 512)
                bk = bank('A')
                mm(PS(bk), wmq[:, h, :], ucT[:, tsl], True, True, ['wmq', 'ucT'], [pk(bk)])
                cp('vector', qT[:, tsl], PS(bk), [pk(bk)], ['qT'])
                bk = bank('A')
                mm(PS(bk), wmk[:, h, :], ucT[:, tsl], True, True, ['wmk', 'ucT'], [pk(bk)])
                act(kT[:, tsl], PS(bk), AF.Copy, [pk(bk)], ['kT'], scale=128.0 ** -0.5)
                bk = bank('A')
                for i4 in range(4):
                    i = 4 * blk + i4
                    mm(PS(bk)[:, 128 * i4:128 * i4 + 128], ucT[:, 128 * i:128 * i + 128], wmk[:, h, :], True, True,
                       ['wmk', 'ucT'], [pk(bk)])
                act(ktok[:, 4 * blk:4 * blk + 4, :], PS(bk).rearrange('p (a b) -> p a b', a=4), AF.Copy, [pk(bk)], ['ktok'],
                    scale=128.0 ** -0.5)

            qk_at = {6: 0, 9: 1, 12: 2}
            for i in range(16):
                tsl = slice(128 * i, 128 * i + 128)
                bk = bank('A')
                for kc in range(8):
                    mm(PS(bk)[:, 0:256], hT[:, kc, tsl], wu[slot][:, kc, 128:384], kc == 0, kc == 7, ['hT'] + wuk, [pk(bk)])
                act(Vp[0][:, i, 0:128], PS(bk)[:, 0:128], AF.Copy, [pk(bk), 'ea0'], ['Vp0'], scale=ea[0][:, 4 * i + h:4 * i + h + 1])
                act(sigo[:, i, :], PS(bk)[:, 128:256], AF.Sigmoid, [pk(bk)], ['sigo'])
                ts('vector', Vp[1][:, i, 0:128], PS(bk)[:, 0:128], ea[1][:, 4 * i + h:4 * i + h + 1], None, ALU.mult, None,
                   [pk(bk), 'ea1', 'sigo', 'Vp0'], ['Vp1'])
                if i < 12:
                    conv_op(i)
                if i in qk_at:
                    qk_block(qk_at[i])
            qk_block(3)
            for d in range(2):
                cp('vector', Vp[d][:, :, 128:129], ea[d].rearrange('p (a b) -> p a b', a=16)[:, :, h:h + 1], ['ea%d' % d], ['Vp%d' % d])
            if b_ == 0 and h == 0:
                dump('ucT', ucT[:, 0:512], 'ucT')
                dump('qT', qT[:, 0:512], 'qT')
                dump('kT', kT[:, 0:512], 'kT')
                dump('ktok', ktok[:, 0:4, :], 'ktok')
                dump('Vp0', Vp[0][:, 0:4, :], 'Vp0')
                dump('sigo', sigo[:, 0:4, :], 'sigo')
            orders = [list(range(0, 15)), list(range(15, 0, -1))]
            prevs = [None, None]
            for k in range(15):
                for d in range(2):
                    c = orders[d][k]
                    prev = prevs[d]
                    bk = bank('B')
                    mm(PS(bk)[:, 0:129], ktok[:, c, :], Vp[d][:, c, :], True, True, ['ktok', 'Vp%d' % d], [pk(bk)])
                    if prev is None:
                        cp('vector', Y[d], PS(bk)[:, 0:129], [pk(bk)], ['Y%d' % d])
                    else:
                        stt(Y[d], Y[d], eg[d][:, 4 * prev + h:4 * prev + h + 1], PS(bk)[:, 0:129], ALU.mult, ALU.add,
                            ['Y%d' % d, 'eg%d' % d, pk(bk)], ['Y%d' % d])
                    act(Cbf[d][:, c, :], Y[d], AF.Copy, ['Y%d' % d, 'eg%d' % d], ['Cbf%d' % d], scale=eg[d][:, 4 * c + h:4 * c + h + 1])
                    prevs[d] = c
            ebh = [eb[d].rearrange('p (c g) -> p c g', g=4)[:, :, h] for d in range(2)]
            hsS = u_sb[:, 0:2048].rearrange('p (c e) -> p c e', c=16)

            def st_and_masks(c):
                csl = slice(128 * c, 128 * c + 128)
                sbk = bank('A')
                mm(PS(sbk)[:, 0:128], kT[:, csl], qT[:, csl], True, True, ['kT', 'qT'], [pk(sbk)])
                tt('vector', PTm[2 * (c % 2)], PS(sbk)[:, 0:128], masku, ALU.mult, [pk(sbk), 'masku'], ['PTm%d' % (2 * (c % 2))])
                tt('vector', PTm[2 * (c % 2) + 1], PS(sbk)[:, 0:128], maskl, ALU.mult, [pk(sbk), 'maskl'], ['PTm%d' % (2 * (c % 2) + 1)])

            dbk = bank('B')
            st_and_masks(0)
            for c in range(16):
                csl = slice(128 * c, 128 * c + 128)
                if c + 1 < 16:
                    st_and_masks(c + 1)
                for d in range(2):
                    nb = c - 1 if d == 0 else c + 1
                    has = 0 <= nb <= 15
                    pt = PTm[2 * (c % 2) + d]
                    ptk = 'PTm%d' % (2 * (c % 2) + d)
                    col = 16 * d + c
                    mm(PS(dbk)[:, col:col + 1], pt, Vp[d][:, c, 128:129], True, not has, [ptk, 'Vp%d' % d], [pk(dbk)])
                    if has:
                        mm(PS(dbk)[:, col:col + 1], qT[:, csl], Cbf[d][:, nb, 128:129], False, True, ['qT', 'Cbf%d' % d], [pk(dbk)])
            for d in range(2):
                tt('vector', dn[:, 16 * d:16 * d + 16], PS(dbk)[:, 16 * d:16 * d + 16], ebh[d], ALU.mult, [pk(dbk), 'eb%d' % d], ['dn'])
            ts('vector', dneg, dn, -1.0, None, ALU.mult, None, ['dn'], ['dneg'])
            tt('vector', dn, dn, onesf[:, 0:32], ALU.max, ['dn', 'onesf'], ['dn'])
            tt('vector', dn, dn, dneg, ALU.max, ['dn', 'dneg'], ['dn'])
            recip(dn, dn, ['dn'], ['dn'])
            for d in range(2):
                tt('vector', rr[:, 16 * d:16 * d + 16], dn[:, 16 * d:16 * d + 16], ebh[d], ALU.mult, ['dn', 'eb%d' % d], ['rr'])
            st_and_masks(0)
            for c in range(16):
                csl = slice(128 * c, 128 * c + 128)
                if c + 1 < 16:
                    st_and_masks(c + 1)
                ab = []
                for d in range(2):
                    bk = bank('B')
                    ab.append(bk)
                    nb = c - 1 if d == 0 else c + 1
                    has = 0 <= nb <= 15
                    pt = PTm[2 * (c % 2) + d]
                    ptk = 'PTm%d' % (2 * (c % 2) + d)
                    mm(PS(bk)[:, 0:128], pt, Vp[d][:, c, 0:128], True, not has, [ptk, 'Vp%d' % d], [pk(bk)])
                    if has:
                        mm(PS(bk)[:, 0:128], qT[:, csl], Cbf[d][:, nb, 0:128], False, True, ['qT', 'Cbf%d' % d], [pk(bk)])
                act(hsS[:, c, :], PS(ab[0])[:, 0:128], AF.Copy, [pk(ab[0]), 'rr'], ['u_sb'], scale=rr[:, c:c + 1])
                stt(hsS[:, c, :], PS(ab[1])[:, 0:128], rr[:, 16 + c:17 + c], hsS[:, c, :], ALU.mult, ALU.add,
                    [pk(ab[1]), 'rr', 'u_sb'], ['u_sb'])
            sqv = ucT.rearrange('p (c e) -> p c e', c=16)
            act(sqv, hsS, AF.Square, ['u_sb'], ['ucT'])
            S.op('vector', lambda e, hsS=hsS, s1=s1: e.tensor_reduce(out=s1, in_=hsS, axis=mybir.AxisListType.X, op=ALU.add), ['u_sb'], ['s1'])
            ts('vector', s1, s1, 1.0 / 128.0, None, ALU.mult, None, ['s1'], ['s1'])
            tt('vector', dneg[:, 0:16], s1, s1, ALU.mult, ['s1'], ['dneg'])
            S.op('vector', lambda e, sqv=sqv, s2=s2: e.tensor_reduce(out=s2, in_=sqv, axis=mybir.AxisListType.X, op=ALU.add), ['ucT'], ['s2'])
            stt(s2, s2, 1.0 / 128.0, dneg[:, 0:16], ALU.mult, ALU.subtract, ['s2', 'dneg'], ['s2'])
            act(rstd, s2, AF.Sqrt, ['s2', 'epsc'], ['rstd'], scale=1.0, bias=epsc)
            recip(rstd, rstd, ['rstd'], ['rstd'])
            stt(dneg[:, 16:32], s1, -1.0, rstd, ALU.mult, ALU.mult, ['s1', 'rstd'], ['dneg'])
            if b_ == 0 and h == 0:
                dump('hs', hsS[:, 1, :], 'u_sb')
                dump('rr0', rr[:, 1:2], 'rr')
                dump('rr1', rr[:, 17:18], 'rr')
            tbs = {}

            def ln_a(c):
                act(hn[c % 2], hsS[:, c, :], AF.Identity, ['u_sb', 'rstd', 'dneg'], ['hn%d' % (c % 2)],
                    scale=rstd[:, c:c + 1], bias=dneg[:, 16 + c:17 + c])
                tt('vector', mem[c % 2], hn[c % 2], sigo[:, c, :], ALU.mult, ['hn%d' % (c % 2), 'sigo'], ['mem%d' % (c % 2)])
                tb = bank('A')
                tbs[c] = tb
                tr(PS(tb)[:, 0:128], mem[c % 2], identf, ['mem%d' % (c % 2), 'identf'], [pk(tb)])

            def ln_b(c):
                csl = slice(128 * c, 128 * c + 128)
                tb = tbs[c]
                act(mixT[:, h, csl], PS(tb)[:, 0:128], AF.Copy, [pk(tb), 'gngT'], ['mixT'], scale=gngT[:, h:h + 1])

            for c in range(18):
                if c < 16:
                    ln_a(c)
                if c >= 2:
                    ln_b(c - 2)
        if b_ == 0:
            dump('mixM', mixT[:, :, 0:256], 'mixT')
        S.barrier()
        cur[0] = unit_base
        wo = alloc([128, 4, 1024], BF16)
        load_wo(1)
        apply_wout()
        S.barrier()
        if b_ == 0:
            dump('x1', xT[:, :, 0:256], 'xT')
        if stop_after == 'mixer':
            break

        cur[0] = phase_base
        for _ in norm_to_hT(1):
            pass
        actT = alloc([128, 22, 1024], BF16)
        wgu = [alloc([128, 8, 256], BF16) for _ in range(2)]
        wd = [alloc([128, 22, 128], BF16) for _ in range(2)]
        g_sb = [alloc([128, 1026], F32) for _ in range(2)]
        ftmp = alloc([128, 1024], F32)
        halo_sb = alloc([128, 22], F32)
        ge = [alloc([128, 1024], BF16) for _ in range(2)]

        def load_wgu(f):
            ldc(wgu[f % 2].rearrange('p a b -> p (a b)'), wgu_d[f].rearrange('p a b -> p (a b)'), 'wgu%d' % (f % 2), ['wgu%d' % (f % 2)])

        def load_wd(n):
            for hh in range(2):
                ldc(wd[n % 2][:, 11 * hh:11 * hh + 11, :].rearrange('p a b -> p (a b)'),
                    wfd_d[n, :, 11 * hh:11 * hh + 11, :].rearrange('p a b -> p (a b)'),
                    'wd%d_%d' % (n % 2, hh), ['wd%d_%d' % (n % 2, hh)])

        for half in range(2):
            t0 = 1024 * half
            load_wgu(0)
            for f in range(22):
                if f < 21:
                    load_wgu(f + 1)
                elif True:
                    load_wd(0)
                w = wgu[f % 2]
                wk = 'wgu%d' % (f % 2)
                gs = g_sb[f % 2]
                gk = 'g_sb%d' % (f % 2)
                for b2 in range(2):
                    tsl = slice(t0 + 512 * b2, t0 + 512 * b2 + 512)
                    bk = bank('A')
                    for kc in range(8):
                        mm(PS(bk), w[:, kc, 0:128], hT[:, kc, tsl], kc == 0, kc == 7, ['hT', wk], [pk(bk)])
                    cp('scalar', gs[:, 1 + 512 * b2:1 + 512 * b2 + 512], PS(bk), [pk(bk)], [gk])
                if half == 0:
                    bk = bank('A')
                    for kc in range(8):
                        mm(PS(bk)[:, 0:2], w[:, kc, 0:128], hT[:, kc, 1023:1025], kc == 0, kc == 7, ['hT', wk], [pk(bk)])
                    memset('vector', gs[:, 0:1], 0.0, [gk])
                    cp('vector', gs[:, 1025:1026], PS(bk)[:, 1:2], [pk(bk)], [gk])
                    cp('vector', halo_sb[:, f:f + 1], PS(bk)[:, 0:1], [pk(bk)], ['halo_sb'])
                else:
                    cp('vector', gs[:, 0:1], halo_sb[:, f:f + 1], ['halo_sb'], [gk])
                    memset('vector', gs[:, 1025:1026], 0.0, [gk])
                ts('vector', ftmp, gs[:, 1:1025], fcw[:, f, 1:2], fcb[:, f:f + 1], ALU.mult, ALU.add, [gk, 'fcw', 'fcb'], ['ftmp'])
                stt(ftmp, gs[:, 0:1024], fcw[:, f, 0:1], ftmp, ALU.mult, ALU.add, [gk, 'fcw', 'ftmp'], ['ftmp'])
                stt(ftmp, gs[:, 2:1026], fcw[:, f, 2:3], ftmp, ALU.mult, ALU.add, [gk, 'fcw', 'ftmp'], ['ftmp'])
                gb_ = ge[f % 2]
                gek = 'ge%d' % (f % 2)
                act(gb_, ftmp, AF.Gelu, ['ftmp'], [gek])
                for b2 in range(2):
                    tsl = slice(t0 + 512 * b2, t0 + 512 * b2 + 512)
                    bk = bank('B')
                    for kc in range(8):
                        mm(PS(bk), w[:, kc, 128:256], hT[:, kc, tsl], kc == 0, kc == 7, ['hT', wk], [pk(bk)])
                    tt('vector', actT[:, f, 512 * b2:512 * b2 + 512], PS(bk), gb_[:, 512 * b2:512 * b2 + 512], ALU.mult,
                       [pk(bk), gek], ['actT'])
            for n in range(8):
                if n < 7:
                    load_wd(n + 1)
                wdk = ['wd%d_0' % (n % 2), 'wd%d_1' % (n % 2)]
                for b2 in range(2):
                    tsl = slice(t0 + 512 * b2, t0 + 512 * b2 + 512)
                    bk = bank('B')
                    for f in range(22):
                        mm(PS(bk), wd[n % 2][:, f, :], actT[:, f, 512 * b2:512 * b2 + 512], f == 0, f == 21, wdk + ['actT'], [pk(bk)])
                    tt('vector', xT[:, n, tsl], PS(bk), xT[:, n, tsl], ALU.add, [pk(bk), 'xT'], ['xT'])
        S.barrier()
        if b_ == 0:
            dump('x2', xT[:, :, 0:256], 'xT')
        if stop_after == 'ffn':
            break

        cur[0] = phase_base
        for _ in norm_to_hT(2):
            pass
        pT = alloc([128, 2, S_LEN], BF16)
        pst = [alloc([128, 256], F32) for _ in range(2)]
        wpp = alloc([128, 2, 1024], BF16)
        wpg = [alloc([128, 8, 128], BF16) for _ in range(2)]
        sg = [alloc([128, 512], F32) for _ in range(2)]
        ptmp = [alloc([128, 512], F32) for _ in range(2)]
        ldc(wpp.rearrange('p a b -> p (a b)'), wpp_d.rearrange('p a b -> p (a b)'), 'wpp', ['wpp'])

        def load_wpg(n):
            ldc(wpg[n % 2].rearrange('p a b -> p (a b)'), wpg_d[n].rearrange('p a b -> p (a b)'), 'wpg%d' % (n % 2), ['wpg%d' % (n % 2)])
        load_wpg(0)
        for i in range(16):
            k = 'pst%d' % (i % 2)
            ld(pst[i % 2], p_d[b_, 128 * i:128 * i + 128, :], k, [k])
            bk = bank('A')
            for c in range(2):
                tr(PS(bk)[:, 128 * c:128 * c + 128], pst[i % 2][:, 128 * c:128 * c + 128], identf, [k, 'identf'], [pk(bk)])
            cp(evac_eng(), pT[:, :, 128 * i:128 * i + 128], PS(bk)[:, 0:256].rearrange('p (c t) -> p c t', c=2), [pk(bk)], ['pT'])
        cnt = 0
        for n in range(8):
            if n < 7:
                load_wpg(n + 1)
            wk = 'wpg%d' % (n % 2)
            for blk in range(4):
                tsl = slice(512 * blk, 512 * blk + 512)
                bk = bank('A')
                for kc in range(8):
                    mm(PS(bk), wpg[n % 2][:, kc, :], hT[:, kc, tsl], kc == 0, kc == 7, ['hT', wk], [pk(bk)])
                sgb = sg[cnt % 2]
                sgk = 'sg%d' % (cnt % 2)
                act(sgb, PS(bk), AF.Sigmoid, [pk(bk), 'bpg'], [sgk], scale=1.0, bias=bpg[:, n:n + 1])
                bk2 = bank('B')
                for k2 in range(2):
                    mm(PS(bk2), wpp[:, k2, 128 * n:128 * n + 128], pT[:, k2, tsl], k2 == 0, k2 == 1, ['wpp', 'pT'], [pk(bk2)])
                pt_ = ptmp[cnt % 2]
                ptk = 'ptmp%d' % (cnt % 2)
                tt('vector', pt_, PS(bk2), sgb, ALU.mult, [pk(bk2), sgk], [ptk])
                tt('vector', xT[:, n, tsl], xT[:, n, tsl], pt_, ALU.add, ['xT', ptk], ['xT'])
                cnt += 1
        S.barrier()
        if b_ == 0:
            dump('x3', xT[:, :, 0:256], 'xT')

        cur[0] = phase_base
        osb = [alloc([128, 1024], F32) for _ in range(2)]
        for blk, r, rk in norm_to_hT(None):
            tsl = slice(512 * blk, 512 * blk + 512)
            xk = 'xTf%d' % blk
            for c in range(8):
                stt(xT[:, c, tsl], xT[:, c, tsl], gains[:, 3, c:c + 1], r, ALU.mult, ALU.mult, ['gains', rk], [xk])
            for i4 in range(4):
                i = 4 * blk + i4
                ob = osb[i % 2]
                ok = 'osb%d' % (i % 2)
                for g in range(2):
                    bk = bank('B')
                    for c4 in range(4):
                        c = 4 * g + c4
                        tr(PS(bk)[:, 128 * c4:128 * c4 + 128], xT[:, c, 128 * i:128 * i + 128], identf, [xk, 'identf'], [pk(bk)])
                    cp(evac_eng(), ob[:, 512 * g:512 * g + 512], PS(bk), [pk(bk)], [ok])
                st(y_d[b_, 128 * i:128 * i + 128, :], ob, 'y%d' % (i % 2), [ok])
        if b_ == nseq - 1:
            S.barrier()
    S.barrier()
    S.emit()
    return nc


def _prep_shared(inp):
    f32 = np.float32
    bf = ml_dtypes.bfloat16
    W_in = np.asarray(inp['w_in'], f32)[0]

    def kc_layout(w):
        n = w.shape[1]
        return np.ascontiguousarray(w.reshape(8, 128, n).transpose(1, 0, 2))
    win_a = np.stack([kc_layout(np.concatenate([W_in[:, 128 * hp:128 * hp + 128], W_in[:, 512 + 128 * hp:512 + 128 * hp + 128],
                                                W_in[:, 1024 + 128 * hp:1024 + 128 * hp + 128]], axis=1)) for hp in range(4)])
    win_m = np.stack([kc_layout(np.concatenate([W_in[:, 1536 + 128 * h:1536 + 128 * h + 128], W_in[:, 2048 + 128 * h:2048 + 128 * h + 128],
                                                W_in[:, 2560 + 128 * h:2560 + 128 * h + 128]], axis=1)) for h in range(4)])
    win_g = kc_layout(W_in[:, 3072:3088])
    wmq = np.ascontiguousarray(np.asarray(inp['w_mq'], f32)[0].transpose(1, 0, 2))
    wmk = np.ascontiguousarray(np.asarray(inp['w_mk'], f32)[0].transpose(1, 0, 2))
    w_out = np.asarray(inp['w_out'], f32)[0]
    wout = np.stack([np.ascontiguousarray(w_out[512 * hf:512 * hf + 512].reshape(4, 128, 1024).transpose(1, 0, 2)) for hf in range(2)])
    wg_ = np.asarray(inp['w_ffn_gate'], f32)[0]
    wu_ = np.asarray(inp['w_ffn_up'], f32)[0]
    wgu = np.stack([kc_layout(np.concatenate([wg_[:, 128 * f:128 * f + 128], wu_[:, 128 * f:128 * f + 128]], axis=1)) for f in range(22)])
    wd_ = np.asarray(inp['w_ffn_down'], f32)[0]
    wfd = np.stack([np.ascontiguousarray(wd_[:, 128 * n:128 * n + 128].reshape(22, 128, 128).transpose(1, 0, 2)) for n in range(8)])
    wpg_ = np.asarray(inp['w_ple_gate'], f32)[0]
    wpg = np.stack([kc_layout(wpg_[:, 128 * n:128 * n + 128]) for n in range(8)])
    wpp_ = np.asarray(inp['w_ple_proj'], f32)[0]
    wpp = np.ascontiguousarray(wpp_.reshape(2, 128, 1024).transpose(1, 0, 2))

    def fm(v, n):
        return np.ascontiguousarray(np.asarray(v, f32).reshape(n, 128).T)
    gains = np.stack([fm(inp['ln_mix_g'][0], 8), fm(inp['ln_ffn_g'][0], 8), fm(inp['ln_ple_g'][0], 8), fm(inp['ln_final_g'], 8)], axis=1)
    bpg = fm(inp['b_ple_gate'][0], 8)
    fcw = np.ascontiguousarray(np.asarray(inp['ffn_conv_w'], f32)[0].reshape(3, 22, 128).transpose(2, 1, 0))
    fcb = fm(inp['ffn_conv_b'][0], 22)
    mcw = np.ascontiguousarray(np.asarray(inp['mlstm_conv_w'], f32)[0].reshape(3, 4, 128).transpose(2, 1, 0))
    mcb = fm(inp['mlstm_conv_b'][0], 4)
    gng = np.ascontiguousarray(np.broadcast_to(np.asarray(inp['mlstm_gn_g'], f32)[0][None, :], (128, 512)))
    gngT = fm(inp['mlstm_gn_g'][0], 4)
    big = np.asarray(inp['b_igate'], f32)[0]
    bfg = np.asarray(inp['b_fgate'], f32)[0]
    gb = np.concatenate([big[0], big[1], bfg[0], bfg[1]])
    gbias = np.ascontiguousarray(np.broadcast_to(gb[None, :], (128, 16)))
    inv = (500000.0 ** (-np.arange(0, 16, 2, dtype=np.float32) / np.float32(16))).astype(f32)
    invf = np.ascontiguousarray(np.broadcast_to(inv[None, :], (128, 8)))
    ki = np.arange(128)[:, None]
    col = np.arange(23 * 128)[None, :]
    d = (col // 128 - 11) * 128 + (col % 128) - ki
    ad = np.abs(d)
    c = (ad <= 64).astype(f32) + ((d % 4 == 0) & (ad <= 256)).astype(f32) + ((d % 16 == 0) & (ad <= 1024)).astype(f32)
    ss, tt_ = np.arange(128)[:, None], np.arange(128)[None, :]
    shared = dict(win_a=win_a, win_m=win_m, win_g=win_g, wmq=wmq, wmk=wmk, wout=wout, wgu=wgu, wfd=wfd, wpg=wpg, wpp=wpp,
                  gains=np.ascontiguousarray(gains), bpg=bpg, fcw=fcw, fcb=fcb, mcw=mcw, mcb=mcb, gng=gng, gngT=gngT, gbias=gbias, invf=invf,
                  identb=np.eye(128).astype(bf), identf=np.eye(128, dtype=f32), maskc=c.astype(bf),
                  masku=(ss <= tt_).astype(bf), maskl=(ss >= tt_).astype(bf),
                  triu=(ss <= tt_).astype(f32), tril=(ss >= tt_).astype(f32))
    return shared


def _core_maps(inp, shared, ncores=8):
    x = np.asarray(inp['x'], np.float32)
    p = np.asarray(inp['p'], np.float32)[0]
    pos = np.asarray(inp['positions'], np.int32)
    maps = []
    for c in range(ncores):
        m = dict(shared)
        m['x'] = np.ascontiguousarray(x[2 * c:2 * c + 2])
        m['p'] = np.ascontiguousarray(p[2 * c:2 * c + 2])
        m['pos'] = np.ascontiguousarray(pos[2 * c:2 * c + 2].reshape(2, 16, 128).transpose(0, 2, 1))
        maps.append(m)
    return maps


def kernel(**inputs):
    shared = _prep_shared(inputs)
    maps = _core_maps(inputs, shared)
    nc = build_program()
    res = run_bass_kernel_spmd(nc, maps, core_ids=list(range(8)))
    out = np.concatenate([np.asarray(r['y'], np.float32) for r in res.results], axis=0)
    return out
```
